# Optimizing a Trainium2 kernel written in Bass

```python
import math
import jax, jax.numpy as jnp
from jax import lax
import numpy as np

D_MODEL = 1024
BATCH = 4
SEQ = 8192
DEPTH = 1

MEM_LEN = 256
HEAD_DIM = 64
N_DIFF_HEADS = 4
DIFF_V_DIM = 2 * HEAD_DIM
N_FOX_HEADS = 8
N_MEM_HEADS = 4
MEM_HEAD_DIM = D_MODEL // N_MEM_HEADS
D_FF = 4 * D_MODEL
N_BUCKETS = 32
MAX_DISTANCE = 128
Q_BLOCK = 128
EPS = 1e-6

DIFF_QK_W = N_DIFF_HEADS * 2 * HEAD_DIM
DIFF_V_W = N_DIFF_HEADS * DIFF_V_DIM
FOX_W = N_FOX_HEADS * HEAD_DIM
IN_SPLITS = [DIFF_QK_W, DIFF_QK_W, DIFF_V_W, FOX_W, FOX_W, FOX_W, N_FOX_HEADS, D_MODEL, D_MODEL]
IN_COLS = int(sum(IN_SPLITS))
IN_OFFSETS = [int(v) for v in np.cumsum(IN_SPLITS)[:-1]]

kernel_name = "hybrid_diff_fox_gated_block"


def _rmsnorm(x, g):
    xf = x.astype(jnp.float32)
    y = xf * lax.rsqrt(jnp.mean(xf * xf, axis=-1, keepdims=True) + EPS)
    return (y * g.astype(jnp.float32)).astype(x.dtype)


def _split_heads(t, n_heads):
    b, s, _ = t.shape
    return t.reshape(b, s, n_heads, -1).transpose(0, 2, 1, 3)


def _merge_blocks(o):
    nb, b, h, qb, d = o.shape
    return o.transpose(1, 0, 3, 2, 4).reshape(b, nb * qb, h, d)


def _t5_bucket(dist):
    dist = jnp.maximum(dist, 0)
    max_exact = N_BUCKETS // 2
    d = jnp.maximum(dist, 1).astype(jnp.float32)
    large = max_exact + (jnp.log(d / max_exact) / math.log(MAX_DISTANCE / max_exact)
                         * (N_BUCKETS - max_exact)).astype(jnp.int32)
    large = jnp.minimum(large, N_BUCKETS - 1)
    return jnp.where(dist < max_exact, dist, large)


def _hybrid_mixer(h, w_in, b_forget, lq1, lk1, lq2, lk2, g_subln, rel_bias,
                  w_diff_out, w_fox_out, w_o, lambda_init):
    b, s, _ = h.shape
    proj = h @ w_in
    qa, ka, va, qb, kb, vb, fl, ga, gb = jnp.split(proj, IN_OFFSETS, axis=-1)

    qa = _split_heads(qa, 2 * N_DIFF_HEADS).reshape(b, N_DIFF_HEADS, 2, s, HEAD_DIM)
    ka = _split_heads(ka, 2 * N_DIFF_HEADS).reshape(b, N_DIFF_HEADS, 2, s, HEAD_DIM)
    qa1, qa2 = qa[:, :, 0], qa[:, :, 1]
    ka1, ka2 = ka[:, :, 0], ka[:, :, 1]
    va = _split_heads(va, N_DIFF_HEADS)
    lam = (jnp.exp(jnp.sum(lq1.astype(jnp.float32) * lk1.astype(jnp.float32)))
           - jnp.exp(jnp.sum(lq2.astype(jnp.float32) * lk2.astype(jnp.float32)))
           + lambda_init)

    qb = _split_heads(qb, N_FOX_HEADS)
    kb = _split_heads(kb, N_FOX_HEADS)
    vb = _split_heads(vb, N_FOX_HEADS)
    log_f = jax.nn.log_sigmoid(fl.astype(jnp.float32) + b_forget.astype(jnp.float32))
    cum = jnp.cumsum(log_f, axis=1).transpose(0, 2, 1)

    scale = HEAD_DIM ** -0.5
    table = rel_bias.astype(jnp.float32)
    k_pos = jnp.arange(s)

    def block(i):
        q0 = i * Q_BLOCK
        q_pos = q0 + jnp.arange(Q_BLOCK)
        dist = q_pos[:, None] - k_pos[None, :]
        causal = dist >= 0
        rel = jnp.transpose(table[_t5_bucket(dist)], (2, 0, 1))
        sl = lambda t: lax.dynamic_slice_in_dim(t, q0, Q_BLOCK, axis=2)
        s1 = jnp.einsum('bhqd,bhkd->bhqk', sl(qa1), ka1).astype(jnp.float32) * scale + rel
        s2 = jnp.einsum('bhqd,bhkd->bhqk', sl(qa2), ka2).astype(jnp.float32) * scale + rel
        a1 = jax.nn.softmax(jnp.where(causal, s1, -jnp.inf), axis=-1)
        a2 = jax.nn.softmax(jnp.where(causal, s2, -jnp.inf), axis=-1)
        oa = jnp.einsum('bhqk,bhkd->bhqd', (a1 - lam * a2).astype(va.dtype), va)
        c_q = lax.dynamic_slice_in_dim(cum, q0, Q_BLOCK, axis=2)
        sb = (jnp.einsum('bhqd,bhkd->bhqk', sl(qb), kb).astype(jnp.float32) * scale
              + (c_q[..., None] - cum[:, :, None, :]))
        pb = jax.nn.softmax(jnp.where(causal, sb, -jnp.inf), axis=-1)
        ob = jnp.einsum('bhqk,bhkd->bhqd', pb.astype(vb.dtype), vb)
        return oa, ob

    oa, ob = lax.map(block, jnp.arange(s // Q_BLOCK))
    ya = _rmsnorm(_merge_blocks(oa), g_subln) * (1.0 - lambda_init)
    ya = ya.reshape(b, s, DIFF_V_W)
    yb = _merge_blocks(ob).reshape(b, s, FOX_W)
    merged = jax.nn.sigmoid(ga) * (ya @ w_diff_out) + jax.nn.sigmoid(gb) * (yb @ w_fox_out)
    return merged @ w_o


def _memory_cross_attention(h, mem_n, w_q, w_kv, w_o):
    b, s, _ = h.shape
    q = _split_heads(h @ w_q, N_MEM_HEADS)
    k, v = jnp.split(mem_n @ w_kv, 2, axis=-1)
    k = _split_heads(k, N_MEM_HEADS)
    v = _split_heads(v, N_MEM_HEADS)
    sc = jnp.einsum('bhqd,bhkd->bhqk', q, k).astype(jnp.float32) * (MEM_HEAD_DIM ** -0.5)
    p = jax.nn.softmax(sc, axis=-1).astype(v.dtype)
    o = jnp.einsum('bhqk,bhkd->bhqd', p, v).transpose(0, 2, 1, 3).reshape(b, s, D_MODEL)
    return o @ w_o


def _sq_relu_mlp(h, w1, w2):
    a = jax.nn.relu(h @ w1)
    return (a * a) @ w2


def setup_inputs(seed: int = 0) -> dict:
    key = jax.random.key(seed)
    ks = jax.random.split(key, 24)
    n = lambda k, shape, s: jax.random.normal(k, shape, jnp.float32) * s
    gain = lambda k, shape: 1.0 + 0.02 * jax.random.normal(k, shape, jnp.float32)
    L = DEPTH
    return {
        "x": n(ks[0], (BATCH, SEQ, D_MODEL), 1.0),
        "mem": n(ks[1], (BATCH, MEM_LEN, D_MODEL), 1.0),
        "w_in": n(ks[2], (L, D_MODEL, IN_COLS), D_MODEL ** -0.5),
        "b_forget": 3.0 + n(ks[3], (L, N_FOX_HEADS), 1.0),
        "lambda_q1": n(ks[4], (L, HEAD_DIM), 0.1),
        "lambda_k1": n(ks[5], (L, HEAD_DIM), 0.1),
        "lambda_q2": n(ks[6], (L, HEAD_DIM), 0.1),
        "lambda_k2": n(ks[7], (L, HEAD_DIM), 0.1),
        "g_subln": gain(ks[8], (L, DIFF_V_DIM)),
        "rel_bias": n(ks[9], (N_BUCKETS, N_DIFF_HEADS), 0.5),
        "w_diff_out": n(ks[10], (L, DIFF_V_W, D_MODEL), DIFF_V_W ** -0.5),
        "w_fox_out": n(ks[11], (L, FOX_W, D_MODEL), FOX_W ** -0.5),
        "w_o": n(ks[12], (L, D_MODEL, D_MODEL), D_MODEL ** -0.5),
        "g_mix": gain(ks[13], (L, D_MODEL)),
        "g_mem_q": gain(ks[14], (L, D_MODEL)),
        "g_mem_kv": gain(ks[15], (L, D_MODEL)),
        "w_q_mem": n(ks[16], (L, D_MODEL, D_MODEL), D_MODEL ** -0.5),
        "w_kv_mem": n(ks[17], (L, D_MODEL, 2 * D_MODEL), D_MODEL ** -0.5),
        "w_o_mem": n(ks[18], (L, D_MODEL, D_MODEL), D_MODEL ** -0.5),
        "g_mlp": gain(ks[19], (L, D_MODEL)),
        "w1": n(ks[20], (L, D_MODEL, D_FF), D_MODEL ** -0.5),
        "w2": n(ks[21], (L, D_FF, D_MODEL), D_FF ** -0.5),
        "g_final": gain(ks[22], (D_MODEL,)),
    }


def reference(x, mem, w_in, b_forget, lambda_q1, lambda_k1, lambda_q2, lambda_k2, g_subln,
              rel_bias, w_diff_out, w_fox_out, w_o, g_mix, g_mem_q, g_mem_kv, w_q_mem,
              w_kv_mem, w_o_mem, g_mlp, w1, w2, g_final):
    for l in range(DEPTH):
        lambda_init = 0.8 - 0.6 * math.exp(-0.3 * l)
        x = x + _hybrid_mixer(_rmsnorm(x, g_mix[l]), w_in[l], b_forget[l], lambda_q1[l],
                              lambda_k1[l], lambda_q2[l], lambda_k2[l], g_subln[l], rel_bias,
                              w_diff_out[l], w_fox_out[l], w_o[l], lambda_init)
        x = x + _memory_cross_attention(_rmsnorm(x, g_mem_q[l]), _rmsnorm(mem, g_mem_kv[l]),
                                        w_q_mem[l], w_kv_mem[l], w_o_mem[l])
        x = x + _sq_relu_mlp(_rmsnorm(x, g_mlp[l]), w1[l], w2[l])
    return _rmsnorm(x, g_final)
```

```python
import math
from contextlib import ExitStack

import numpy as np
import concourse.bass as bass
import concourse.mybir as mybir
from concourse.bass_utils import run_bass_kernel_spmd

F32 = mybir.dt.float32
BF16 = mybir.dt.bfloat16
AF = mybir.ActivationFunctionType
ALU = mybir.AluOpType

S = 8192
D = 1024
NQ = 4096
INC = 5128
EPS = 1e-6
MASKV = -30000.0
LW = 15 * 128

O_QA, O_KA, O_VA, O_QB, O_KB, O_VB, O_FL, O_GA, O_GB = 0, 512, 1024, 1536, 2048, 2560, 3072, 3080, 4104


def own_tiles(r):
    out = []
    for j in range(8):
        g = j // 2
        if r == 0:
            out.append(4 * g + (0 if j % 2 == 0 else 3))
        else:
            out.append(4 * g + (1 if j % 2 == 0 else 2))
    return out


def nk_tiles(j):
    g = j // 2
    return 4 * g + 2 if j % 2 == 0 else 4 * g + 4


class Buf:
    __slots__ = ("w", "r", "name")

    def __init__(self, name=""):
        self.w = None
        self.r = {}
        self.name = name


class DSem:
    __slots__ = ("h", "val", "name")

    def __init__(self, h, name):
        self.h = h
        self.val = 0
        self.name = name


class Prog:
    ENG = ("sp", "pe", "act", "dve", "pool")

    def __init__(self, nc, stack):
        self.nc = nc
        self.stack = stack
        self.q = {e: [] for e in self.ENG}
        self.sem = {e: stack.enter_context(nc.semaphore("s_" + e)) for e in ("pe", "act", "dve", "pool")}
        self.cnt = {e: 0 for e in self.ENG}
        self.seen = {e: {} for e in self.ENG}
        self.dsems = []

    def dsem(self, name):
        s = DSem(self.stack.enter_context(self.nc.semaphore(name)), name)
        self.dsems.append(s)
        return s

    def wait(self, eng, t):
        key, val = t
        if key == eng and (eng == "pe" or val > self.cnt[eng]):
            return
        if self.seen[eng].get(key, 0) >= val:
            return
        self.seen[eng][key] = val
        self.q[eng].append(("wait", key, val))

    def _deps(self, eng, reads, writes):
        for b in reads:
            if b.w is not None:
                self.wait(eng, b.w)
        for b in writes:
            if b.w is not None:
                self.wait(eng, b.w)
            for k, v in b.r.items():
                self.wait(eng, (k, v))

    @staticmethod
    def _upd(t, reads, writes):
        k, v = t
        for b in reads:
            if b.r.get(k, 0) < v:
                b.r[k] = v
        for b in writes:
            b.w = t
            b.r = {}

    def op(self, eng, fn, reads=(), writes=(), mark=True):
        self._deps(eng, reads, writes)
        if mark:
            self.cnt[eng] += 1
            t = (eng, self.cnt[eng])
        else:
            t = (eng, self.cnt[eng] + 1)
        self.q[eng].append(("op", fn, mark))
        self._upd(t, reads, writes)
        return t

    def dma(self, out_ap, in_ap, sem, reads=(), writes=(), q="sp"):
        self._deps(q, reads, writes)
        sem.val += 16
        t = (sem, sem.val)
        self.q[q].append(("dma", out_ap, in_ap, sem))
        self._upd(t, reads, writes)
        return t

    def dma_multi(self, pairs, sem, reads=(), writes=(), q="sp"):
        self._deps(q, reads, writes)
        for o, i in pairs:
            sem.val += 16
            self.q[q].append(("dma", o, i, sem))
        t = (sem, sem.val)
        self._upd(t, reads, writes)
        return t

    def barrier(self):
        ts = [(e, self.cnt[e]) for e in ("pe", "act", "dve", "pool") if self.cnt[e] > 0]
        ts += [(s, s.val) for s in self.dsems if s.val > 0]
        for e in self.ENG:
            for t in ts:
                self.wait(e, t)

    def emit(self):
        nc = self.nc
        for e in self.ENG:
            for it in self.q[e]:
                if it[0] == "wait" and isinstance(it[1], str):
                    assert it[2] <= self.cnt[it[1]], ("unreachable wait", e, it[1], it[2], self.cnt[it[1]])
        with nc.allow_low_precision("bf16 matmul operands / fp32 accumulation by design"), nc.Block() as block:
            decos = {"sp": block.sync, "pe": block.tensor, "act": block.scalar,
                     "dve": block.vector, "pool": block.gpsimd}
            for e in self.ENG:
                items = self.q[e]

                def body(engine, items=items, e=e):
                    for it in items:
                        if it[0] == "wait":
                            key = it[1]
                            h = self.sem[key] if isinstance(key, str) else key.h
                            engine.wait_ge(h, it[2])
                        elif it[0] == "op":
                            ins = it[1](engine)
                            if it[2]:
                                ins.then_inc(self.sem[e], 1)
                        else:
                            _, o, i, s = it
                            engine.dma_start(out=o, in_=i).then_inc(s.h, 16)

                decos[e](body)


class Slot:
    __slots__ = ("ap", "buf", "sem")

    def __init__(self, ap, buf, sem=None):
        self.ap = ap[:]
        self.buf = buf
        self.sem = sem


class Ring:
    def __init__(self, items):
        self.items = items
        self.i = 0

    def next(self):
        it = self.items[self.i % len(self.items)]
        self.i += 1
        return it


class K:
    pass


_SB_CNT = [0]


def mk_sb(nc, st):
    cnt = _SB_CNT

    def sb(name, shape, dt):
        cnt[0] += 1
        return st.enter_context(nc.sbuf_tensor(f"{name}_{cnt[0]}", shape, dt))
    return sb


def evac_copy(P, eng, out_ap, in_ap, reads, writes, scale=None):
    if eng == "act":
        if scale is None:
            return P.op("act", lambda e: e.copy(out=out_ap, in_=in_ap), reads=reads, writes=writes)
        return P.op("act", lambda e: e.mul(out=out_ap, in_=in_ap, mul=scale), reads=reads, writes=writes)
    if scale is None:
        return P.op(eng, lambda e: e.tensor_copy(out=out_ap, in_=in_ap), reads=reads, writes=writes)
    return P.op(eng, lambda e: e.tensor_scalar(out=out_ap, in0=in_ap, scalar1=scale, scalar2=None, op0=ALU.mult),
                reads=reads, writes=writes)


def norm_block(k, P, x_ap, x_buf, hn, small, inv_n, width):
    ss, lnv, rstd, sbuf_ = small
    P.op("act", lambda e: e.activation(out=hn.ap, in_=x_ap, func=AF.Square, accum_out=ss),
         reads=[x_buf], writes=[sbuf_, hn.buf])
    P.op("act", lambda e: e.activation(out=lnv, in_=ss, func=AF.Ln, scale=inv_n, bias=k.epsb[:, 0:1]),
         reads=[sbuf_], writes=[sbuf_])
    P.op("act", lambda e: e.activation(out=rstd, in_=lnv, func=AF.Exp, scale=-0.5), reads=[sbuf_], writes=[sbuf_])
    P.op("act", lambda e: e.activation(out=hn.ap, in_=x_ap, func=AF.Copy, scale=rstd), reads=[x_buf, sbuf_], writes=[hn.buf])


def transpose_block(k, P, src_ap, src_buf, nchunk, bank, dst_ap, dst_buf, eng):
    psb = k.ps[:, bank, :].bitcast(BF16)
    for c in range(nchunk):
        P.op("pe", lambda e, c=c: e.transpose(out=psb[:, c * 128:(c + 1) * 128], in_=src_ap[:, c * 128:(c + 1) * 128],
                                              identity=k.idb[:]),
             reads=[src_buf], writes=[k.PB[bank]], mark=(c == nchunk - 1))
    src = psb[:, 0:nchunk * 128].rearrange("p (c t) -> p c t", c=nchunk)
    evac_copy(P, eng, dst_ap, src, [k.PB[bank]], [dst_buf])


def convert_weight(k, P, w_dram, nrow_chunks, ncols, dst, gT, stg_ring, t_list, engines=("dve", "act")):
    i = 0
    pw = stg_ring.items[0].ap.shape[-1]
    for kc in range(nrow_chunks):
        for c0 in range(0, ncols, pw):
            cw = min(pw, ncols - c0)
            sg = stg_ring.next()
            P.dma(sg.ap[:, 0:cw], w_dram[kc * 128:(kc + 1) * 128, c0:c0 + cw], sg.sem, writes=[sg.buf])
            i += 1
            dst_ap = dst[:, kc, c0:c0 + cw]
            src_ap = sg.ap[:, 0:cw]
            if gT is None:
                if engines[i % len(engines)] == "dve":
                    t = P.op("dve", lambda e, d=dst_ap, s=src_ap: e.tensor_copy(out=d, in_=s), reads=[sg.buf])
                else:
                    t = P.op("act", lambda e, d=dst_ap, s=src_ap: e.copy(out=d, in_=s), reads=[sg.buf])
            else:
                gs = gT[:, kc:kc + 1]
                t = P.op("act", lambda e, d=dst_ap, s=src_ap, gs=gs: e.activation(out=d, in_=s, func=AF.Copy, scale=gs), reads=[sg.buf])
            t_list.append(t)


def wait_all(P, engs, tickets):
    for e in engs:
        for t in tickets:
            P.wait(e, t)


def phase1(k, P):
    nc = k.nc
    io = k.io
    with ExitStack() as st_outer:
      flT = mk_sb(nc, st_outer)("flT", [8, S], F32)
      B_fl = Buf()
      with ExitStack() as st:
        sb = mk_sb(nc, st)
        winb = sb("winb", [128, 8, INC], BF16)
        gmix = sb("gmix", [128, 8], F32)
        s_misc = P.dsem("d_misc1")
        t_g = P.dma(gmix[:], io["g_mixT"], s_misc)
        wait_all(P, ["pool", "dve", "act"], [t_g])
        stg = Ring([Slot(sb("wst", [128, 512], F32), Buf(), P.dsem(f"d_wst{i}")) for i in range(4)])
        stgp = Ring([Slot(sb("wstp", [128, 512], F32), Buf(), P.dsem(f"d_wstp{i}")) for i in range(2)])
        slab_t = {}

        def convert_slab(sl_i, eng):
            c0 = sl_i * 512
            cw = min(512, INC - c0)
            ts = []
            for kc in range(8):
                if eng == "act":
                    sg = stg.next()
                    P.dma(sg.ap[:, 0:cw], io["w_in"][kc * 128:(kc + 1) * 128, c0:c0 + cw], sg.sem, writes=[sg.buf])
                    ts.append(P.op("act", lambda e, sg=sg, kc=kc: e.activation(out=winb[:, kc, c0:c0 + cw], in_=sg.ap[:, 0:cw], func=AF.Copy,
                                                                               scale=gmix[:, kc:kc + 1]), reads=[sg.buf]))
                else:
                    sg = stgp.next()
                    P.dma(sg.ap[:, 0:cw], io["w_in"][kc * 128:(kc + 1) * 128, c0:c0 + cw], sg.sem, writes=[sg.buf], q="pool")
                    ts.append(P.op("pool", lambda e, sg=sg, kc=kc: e.tensor_scalar(out=winb[:, kc, c0:c0 + cw], in0=sg.ap[:, 0:cw],
                                                                                   scalar1=gmix[:, kc:kc + 1], scalar2=None, op0=ALU.mult),
                                   reads=[sg.buf]))
            slab_t[sl_i] = ts

        def need_cols(col0, ncol):
            for sl_i in range(col0 // 512, (col0 + ncol - 1) // 512 + 1):
                for t in slab_t[sl_i]:
                    P.wait("pe", t)

        xblk = Ring([Slot(sb("xblk", [128, 1024], F32), Buf(), P.dsem(f"d_x{i}")) for i in range(3)])
        hns = Ring([Slot(sb("hn", [128, 1024], BF16), Buf()) for _ in range(4)])
        smalls = []
        for i in range(4):
            t_ = sb("sm", [128, 4], F32)
            smalls.append((t_[:, 0:1], t_[:, 1:2], t_[:, 2:3], Buf()))
        smalls = Ring(smalls)
        hTs = Ring([Slot(sb("hT", [128, 8, 512], BF16), Buf()) for _ in range(2)])
        stKd = Slot(sb("stKd", [128, 4, 512], BF16), Buf(), P.dsem("d_stKd"))
        stKf = Slot(sb("stKf", [128, 4, 512], BF16), Buf(), P.dsem("d_stKf"))
        stV = Slot(sb("stV", [128, 4, 1024], BF16), Buf(), P.dsem("d_stV"))
        stQd = Slot(sb("stQd", [128, 4, 512], BF16), Buf(), P.dsem("d_stQd"))
        stQf = Slot(sb("stQf", [128, 4, 512], BF16), Buf(), P.dsem("d_stQf"))
        stSG = Slot(sb("stSG", [128, 8, 2, 512], BF16), Buf(), P.dsem("d_stSG"))
        tb_ring = Ring([0, 1])
        pj_ring = Ring([2, 3, 4, 5, 6, 7])

        blk_list = []
        pending = []

        def issue_load():
            if blk_list:
                src_dram, r0 = blk_list.pop(0)
                sl = xblk.next()
                P.dma(sl.ap[:], src_dram[r0:r0 + 128, :], sl.sem, writes=[sl.buf])
                pending.append(sl)

        def make_hT():
            hT = hTs.next()
            hl = []
            for b in range(4):
                sl = pending.pop(0)
                hn = hns.next()
                norm_block(k, P, sl.ap[:], sl.buf, hn, smalls.next(), 1.0 / D, D)
                issue_load()
                hl.append(hn)
            for b, hn in enumerate(hl):
                transpose_block(k, P, hn.ap, hn.buf, 8, tb_ring.next(), hT.ap[:, :, b * 128:(b + 1) * 128], hT.buf,
                                "act" if b % 2 == 0 else "dve")
            return hT

        def proj_feat(hT, col0, ncol, ntok=512):
            bank = pj_ring.next()
            need_cols(col0, ncol)
            for kc in range(8):
                P.op("pe", lambda e, kc=kc: e.matmul(k.ps[0:ncol, bank, 0:ntok], lhsT=winb[:, kc, col0:col0 + ncol],
                                                     rhs=hT.ap[:, kc, 0:ntok], start=(kc == 0), stop=(kc == 7)),
                     reads=[hT.buf], writes=[k.PB[bank]], mark=(kc == 7))
            return bank

        def proj_tok(hT, blk, col0):
            bank = pj_ring.next()
            need_cols(col0, 512)
            for kc in range(8):
                P.op("pe", lambda e, kc=kc: e.matmul(k.ps[:, bank, :], lhsT=hT.ap[:, kc, blk * 128:(blk + 1) * 128],
                                                     rhs=winb[:, kc, col0:col0 + 512], start=(kc == 0), stop=(kc == 7)),
                     reads=[hT.buf], writes=[k.PB[bank]], mark=(kc == 7))
            return bank

        def kv_tile(t, hT):
            c0 = t * 512
            for h in range(4):
                bank = proj_feat(hT, O_KA + h * 128, 128)
                evac_copy(P, "dve", stKd.ap[:, h, :], k.ps[:, bank, :], [k.PB[bank]], [stKd.buf])
            P.dma(k.KTd[:, :, c0:c0 + 512].rearrange("h p t -> p h t"), stKd.ap[:], stKd.sem, reads=[stKd.buf])
            for h in range(4):
                bank = proj_feat(hT, O_KB + h * 128, 128)
                evac_copy(P, "dve", stKf.ap[:, h, :], k.ps[:, bank, :], [k.PB[bank]], [stKf.buf])
            for half in range(2):
                P.dma(k.KTf[half::2, 0:64, c0:c0 + 512].rearrange("h p t -> p h t"), stKf.ap[half * 64:(half + 1) * 64, :, :],
                      stKf.sem, reads=[stKf.buf])
            bank = proj_feat(hT, O_FL, 8)
            evac_copy(P, "dve", flT[:, c0:c0 + 512], k.ps[0:8, bank, :], [k.PB[bank]], [B_fl])
            for blk in range(4):
                for half, col0 in enumerate((O_VA, O_VB)):
                    bank = proj_tok(hT, blk, col0)
                    evac_copy(P, "dve" if half == 0 else "act", stV.ap[:, blk, half * 512:(half + 1) * 512], k.ps[:, bank, :],
                              [k.PB[bank]], [stV.buf])
            P.dma(k.Vs[c0:c0 + 512, :].rearrange("(b p) c -> p b c", p=128), stV.ap[:], stV.sem, reads=[stV.buf])

        def q_tile(j, hT):
            c0 = j * 512
            for h in range(4):
                bank = proj_feat(hT, O_QA + h * 128, 128)
                evac_copy(P, "dve", stQd.ap[:, h, :], k.ps[:, bank, :], [k.PB[bank]], [stQd.buf], scale=0.125)
            P.dma(k.QTd[:, :, c0:c0 + 512].rearrange("h p t -> p h t"), stQd.ap[:], stQd.sem, reads=[stQd.buf])
            for h in range(4):
                bank = proj_feat(hT, O_QB + h * 128, 128)
                evac_copy(P, "dve", stQf.ap[:, h, :], k.ps[:, bank, :], [k.PB[bank]], [stQf.buf], scale=0.125)
            for half in range(2):
                P.dma(k.QTf[half::2, 0:64, c0:c0 + 512].rearrange("h p t -> p h t"), stQf.ap[half * 64:(half + 1) * 64, :, :],
                      stQf.sem, reads=[stQf.buf])
            for ab, col0 in enumerate((O_GA, O_GB)):
                for c in range(8):
                    bank = proj_feat(hT, col0 + c * 128, 128)
                    evac_copy(P, "dve" if c % 2 == 0 else "act", stSG.ap[:, c, ab, :], k.ps[:, bank, :], [k.PB[bank]], [stSG.buf])
            P.dma(k.SGs[:, :, :, c0:c0 + 512].rearrange("c a p t -> p c a t"), stSG.ap[:], stSG.sem, reads=[stSG.buf])

        order = [("kv", t) for t in range(16)] + [("q", j) for j in range(8)]
        if k.limit_tiles:
            order = order[:k.limit_tiles]
        for kind, idx in order:
            for b in range(4):
                blk_list.append((io["xkv"] if kind == "kv" else io["xq"], idx * 512 + b * 128))
        for _ in range(3):
            issue_load()
        hT_next = make_hT()
        for sl_i in (1, 4, 6, 2, 5):
            convert_slab(sl_i, "act")
        for sl_i in (0, 3, 7, 8, 9, 10):
            convert_slab(sl_i, "pool")
        for oi, (kind, idx) in enumerate(order):
            hT_cur = hT_next
            if oi + 1 < len(order):
                hT_next = make_hT()
            if kind == "kv":
                kv_tile(idx, hT_cur)
            else:
                q_tile(idx, hT_cur)

        P.barrier()
      with ExitStack() as st2:
        if not k.limit_tiles:
            fox_rows(k, P, mk_sb(nc, st2), flT, B_fl)
        P.barrier()


def fox_rows(k, P, sb, flT, B_fl):
    io = k.io
    s_m = P.dsem("d_fox")
    bf = sb("bfg", [8, 2], F32)
    rfl = sb("rfl", [8, 1], F32)
    B_s = Buf()
    s_ld = P.dsem("d_fox_ld")
    s_one = P.dsem("d_fox_one")
    P.dma_multi([(bf[:, 0:1], io["b_forget"]), (rfl[:], io["rflag"])], s_ld, writes=[B_s])
    P.op("dve", lambda e: e.tensor_scalar(out=bf[:, 1:2], in0=bf[:, 0:1], scalar1=-1.0, scalar2=None, op0=ALU.mult),
         reads=[B_s], writes=[B_s])
    ones = sb("ones", [8, 2048], F32)
    onesb = sb("onesb", [8, 4096], BF16)
    B_o = Buf()
    P.op("pool", lambda e: e.memset(ones[:], 1.0), writes=[B_o])
    P.op("pool", lambda e: e.memset(onesb[:], 1.0), writes=[B_o])
    one1 = sb("one1", [8, 1], F32)
    P.op("pool", lambda e: e.memset(one1[:], 1.0), writes=[B_o])
    Chi = sb("Chi", [8, S], BF16)
    BCh = [Buf() for _ in range(S // 2048)]
    CW = 2048
    NCH = S // CW
    tA = [sb("ctA", [8, CW], F32) for _ in range(NCH)]
    tB = [sb("ctB", [8, CW], F32) for _ in range(NCH)]
    tC = [sb("ctC", [8, CW], F32) for _ in range(NCH)]
    cmid = [sb("cmid", [8, CW], BF16) for _ in range(NCH)]
    clo = [sb("clo", [8, CW], BF16) for _ in range(NCH)]
    BA = [Buf() for _ in range(NCH)]
    BB = [Buf() for _ in range(NCH)]
    for ci in range(NCH):
        c0 = ci * CW
        P.op("act", lambda e, c0=c0, ci=ci: e.activation(out=tA[ci][:], in_=flT[:, c0:c0 + CW], func=AF.Exp, scale=-1.0, bias=bf[:, 1:2]),
             reads=[B_fl, B_s], writes=[BA[ci]])
        P.op("act", lambda e, ci=ci: e.activation(out=tA[ci][:], in_=tA[ci][:], func=AF.Ln, bias=one1[:, 0:1]), reads=[BA[ci], B_o], writes=[BA[ci]])
    for ci in range(NCH):
        c0 = ci * CW
        init = 0.0 if ci == 0 else flT[:, c0 - 1:c0]
        P.op("dve", lambda e, c0=c0, init=init, ci=ci: e.tensor_tensor_scan(out=flT[:, c0:c0 + CW], data0=ones[:], data1=tA[ci][:], initial=init,
                                                                         op0=ALU.mult, op1=ALU.add),
             reads=[BA[ci], B_o, B_fl], writes=[B_fl])
        P.op("dve", lambda e, c0=c0: e.tensor_copy(out=Chi[:, c0:c0 + CW], in_=flT[:, c0:c0 + CW]), reads=[B_fl], writes=[BCh[ci]])
        P.op("pool", lambda e, c0=c0, ci=ci: e.tensor_tensor(out=tB[ci][:], in0=flT[:, c0:c0 + CW], in1=Chi[:, c0:c0 + CW], op=ALU.subtract),
             reads=[B_fl, BCh[ci]], writes=[BB[ci]])
        P.op("pool", lambda e, ci=ci: e.tensor_copy(out=cmid[ci][:], in_=tB[ci][:]), reads=[BB[ci]], writes=[BB[ci]])
        P.op("pool", lambda e, ci=ci: e.tensor_tensor(out=tC[ci][:], in0=tB[ci][:], in1=cmid[ci][:], op=ALU.subtract), reads=[BB[ci]], writes=[BB[ci]])
        P.op("pool", lambda e, ci=ci: e.tensor_copy(out=clo[ci][:], in_=tC[ci][:]), reads=[BB[ci]], writes=[BB[ci]])
        P.dma(k.KTf[:, 64, c0:c0 + CW], onesb[:, 0:CW], s_one, reads=[B_o])
        P.dma(k.KTf[:, 65, c0:c0 + CW], Chi[:, c0:c0 + CW], s_m, reads=[BCh[ci]])
        P.dma(k.KTf[:, 66, c0:c0 + CW], cmid[ci][:], s_m, reads=[BB[ci]])
        P.dma(k.KTf[:, 67, c0:c0 + CW], clo[ci][:], s_m, reads=[BB[ci]])
    qrow = sb("qrow", [8, NQ], BF16)
    dtmp = sb("dtmp", [8, 512], F32)
    B_q = Buf()
    ta, tb = own_tiles(0), own_tiles(1)
    for j in range(8):
        a0, b0 = ta[j] * 512, tb[j] * 512
        P.op("dve", lambda e, a0=a0, b0=b0: e.tensor_tensor(out=dtmp[:], in0=Chi[:, b0:b0 + 512], in1=Chi[:, a0:a0 + 512], op=ALU.subtract),
             reads=BCh, writes=[B_q])
        P.op("dve", lambda e, a0=a0: e.scalar_tensor_tensor(out=dtmp[:], in0=dtmp[:], scalar=rfl[:, 0:1], in1=Chi[:, a0:a0 + 512],
                                                            op0=ALU.mult, op1=ALU.add), reads=[B_q, B_s] + BCh, writes=[B_q])
        P.op("dve", lambda e, j=j: e.tensor_scalar(out=qrow[:, j * 512:(j + 1) * 512], in0=dtmp[:], scalar1=-1.0, scalar2=None, op0=ALU.mult),
             reads=[B_q], writes=[B_q])
    P.dma(k.QTf[:, 64, :], qrow[:], s_m, reads=[B_q])
    for rr in (65, 66, 67):
        P.dma(k.QTf[:, rr, :], onesb[:], s_one, reads=[B_o])


def phase2(k, P, Y, B_Y):
    nc = k.nc
    io = k.io
    with ExitStack() as st:
        sb = mk_sb(nc, st)
        G = []
        for i in range(2):
            G.append(dict(KT=sb("KTt", [128, S], BF16)[:], QA=sb("QA", [128, NQ], BF16)[:], QB=sb("QB", [128, NQ], BF16)[:],
                          V=sb("Vt", [128, 64, 130], BF16)[:], L=sb("Lt", [128, 2, LW], F32)[:], t31=sb("t31t", [128, 1], F32)[:],
                          b_k=Buf(), b_q=Buf(), b_v=Buf(), b_l=Buf(), sem=P.dsem(f"d_grp{i}")))
        for gb_ in (G if k.limit_groups else []):
            for nm, bn in (("KT", "b_k"), ("QA", "b_q"), ("QB", "b_q")):
                P.op("dve", lambda e, gb_=gb_, nm=nm: e.memset(gb_[nm], 0.0), writes=[gb_[bn]])
        zerob = sb("zerob", [128, 1], F32)
        tz = P.op("dve", lambda e: e.memset(zerob[:], 0.0))
        P.wait("act", tz)
        Pts = Ring([Slot(sb("Pt", [128, 1024], BF16), Buf()) for _ in range(4)])
        sbanks = Ring([0, 2, 4])
        OB = (6, 7)
        s_m = P.dsem("d_misc2")
        lamt = sb("lamt", [128, 256], F32)
        g8 = sb("g8", [128, 128], F32)
        sm = sb("sm2", [128, 8], F32)
        B_c = Buf()
        P.dma(lamt[:], io["lam4"].rearrange("(o a) d -> o (a d)", o=1).broadcast_to([128, 256]), s_m, writes=[B_c])
        P.dma(g8[:], io["g_subln"].broadcast_to([128, 128]), s_m, writes=[B_c])
        P.op("dve", lambda e: e.tensor_scalar(out=g8[:], in0=g8[:], scalar1=0.8, scalar2=None, op0=ALU.mult), reads=[B_c], writes=[B_c])
        P.op("dve", lambda e: e.tensor_tensor(out=lamt[:, 0:64], in0=lamt[:, 0:64], in1=lamt[:, 64:128], op=ALU.mult), reads=[B_c], writes=[B_c])
        P.op("dve", lambda e: e.tensor_tensor(out=lamt[:, 128:192], in0=lamt[:, 128:192], in1=lamt[:, 192:256], op=ALU.mult), reads=[B_c], writes=[B_c])
        P.op("dve", lambda e: e.reduce_sum(out=sm[:, 0:1], in_=lamt[:, 0:64], axis=mybir.AxisListType.X), reads=[B_c], writes=[B_c])
        P.op("dve", lambda e: e.reduce_sum(out=sm[:, 1:2], in_=lamt[:, 128:192], axis=mybir.AxisListType.X), reads=[B_c], writes=[B_c])
        P.op("act", lambda e: e.activation(out=sm[:, 2:4], in_=sm[:, 0:2], func=AF.Exp), reads=[B_c], writes=[B_c])
        P.op("dve", lambda e: e.tensor_tensor(out=sm[:, 4:5], in0=sm[:, 3:4], in1=sm[:, 2:3], op=ALU.subtract), reads=[B_c], writes=[B_c])
        P.op("dve", lambda e: e.tensor_scalar(out=sm[:, 4:5], in0=sm[:, 4:5], scalar1=-0.2, scalar2=None, op0=ALU.add), reads=[B_c], writes=[B_c])
        neglam = sm[:, 4:5]
        fin = sb("fin", [128, 16], F32)
        B_y1, B_oa, B_fin = Buf(), Buf(), Buf()
        Oraws = Ring([Slot(sb("Oraw", [128, 2, 258], F32), Buf()) for _ in range(2)])

        groups = []
        for h in range(4):
            groups.append(dict(kind="d", h=h, dv=128, lidx=h, ycol=h * 128))
        for h in range(8):
            groups.append(dict(kind="f", h=h, dv=64, lidx=4, ycol=512 + h * 64))
        if k.limit_groups:
            groups = [groups[i] for i in k.limit_groups]

        def load_group(gi):
            g = groups[gi]
            gb = G[gi % 2]
            g["gb"] = gb
            dv = g["dv"]
            h = g["h"]
            P.op("pool", lambda e: e.memset(gb["QA"][64:128, :], 0.0), writes=[gb["b_q"]])
            if g["kind"] == "d":
                P.op("pool", lambda e: e.memset(gb["QB"][0:64, :], 0.0), writes=[gb["b_q"]])
            P.op("pool", lambda e: e.memset(gb["V"][:, :, dv:dv + 1], 1.0), writes=[gb["b_v"]])
            if g["kind"] == "d":
                P.dma(gb["KT"][:, :], k.KTd[h], gb["sem"], writes=[gb["b_k"]])
                P.dma_multi([(gb["QA"][0:64, :], k.QTd[h, 0:64, :]), (gb["QB"][64:128, :], k.QTd[h, 64:128, :])], gb["sem"], writes=[gb["b_q"]])
                vsrc = k.Vs[:, h * 128:(h + 1) * 128]
            else:
                P.dma(gb["KT"][0:68, :], k.KTf[h], gb["sem"], writes=[gb["b_k"]])
                P.dma(gb["QA"][0:68, :], k.QTf[h], gb["sem"], writes=[gb["b_q"]])
                vsrc = k.Vs[:, 512 + h * 64:512 + (h + 1) * 64]
            vsrc = vsrc.rearrange("(kb p) d -> p kb d", p=128)
            P.dma_multi([(gb["V"][:, q8 * 8:(q8 + 1) * 8, 0:dv], vsrc[:, q8 * 8:(q8 + 1) * 8, :]) for q8 in range(8)], gb["sem"], writes=[gb["b_v"]])
            P.dma_multi([(gb["L"][:, :, :], io["Lraw"][g["lidx"]].rearrange("a p w -> p a w")),
                         (gb["t31"][:, :], io["t31"][g["lidx"]:g["lidx"] + 1, :].broadcast_to([128, 1]))], gb["sem"], writes=[gb["b_l"]])
            fin_t = (gb["sem"], gb["sem"].val)
            for bn in ("b_k", "b_q", "b_v", "b_l"):
                gb[bn].w = fin_t

        units = []
        for gi, g in enumerate(groups):
            nmaps = 2 if g["kind"] == "d" else 1
            for j in range(8 if not k.limit_j else k.limit_j):
                nk = nk_tiles(j)
                for mi in range(nmaps):
                    n_u = nk * 2
                    for ui in range(n_u):
                        kt, half = ui // 2, ui % 2
                        nmask = 3 if g["kind"] == "d" else 2
                        m0 = (kt - (nk - 3)) * 4 + half * 2 if kt >= nk - nmask else None
                        units.append(dict(gi=gi, j=j, mi=mi, kb0=kt * 4 + half * 2, m0=m0, first=(ui == 0), last=(ui == n_u - 1),
                                          glast=(ui == n_u - 1 and mi == nmaps - 1 and j == (7 if not k.limit_j else k.limit_j - 1))))

        def O_ap(qb, dv):
            return k.ps[:, OB[qb // 2], (qb % 2) * (dv + 1):(qb % 2 + 1) * (dv + 1)]

        def emit_qk(u):
            g = groups[u["gi"]]
            gb = g["gb"]
            sb0 = sbanks.next()
            u["sb0"] = sb0
            qt = gb["QA"] if u["mi"] == 0 else gb["QB"]
            j = u["j"]
            for i in range(2):
                kb = u["kb0"] + i
                P.op("pe", lambda e, i=i, kb=kb: e.matmul(k.ps[:, sb0 + i, :], lhsT=gb["KT"][:, kb * 128:(kb + 1) * 128],
                                                         rhs=qt[:, j * 512:(j + 1) * 512], start=True, stop=True),
                     reads=[gb["b_k"], gb["b_q"]], writes=[k.PB[sb0 + i]], mark=(i == 1))
            if u["m0"] is not None:
                par = j % 2
                for i in range(2):
                    m = u["m0"] + i
                    off = (11 - m) * 128
                    P.op("dve", lambda e, i=i, off=off: e.tensor_tensor(out=k.ps[:, sb0 + i, :], in0=k.ps[:, sb0 + i, :],
                                                                        in1=gb["L"][:, par, off:off + 512], op=ALU.add),
                         reads=[gb["b_l"]], writes=[k.PB[sb0 + i]])
            pt = Pts.next()
            u["pt"] = pt
            bias_ap = zerob[:, 0:1] if u["m0"] is not None else gb["t31"][:, 0:1]
            P.op("act", lambda e: e.activation(out=pt.ap.rearrange("p (a b) -> p a b", a=2), in_=k.ps[:, sb0:sb0 + 2, :], func=AF.Exp,
                                               bias=bias_ap),
                 reads=[k.PB[sb0], k.PB[sb0 + 1], gb["b_l"]], writes=[pt.buf])

        def emit_av(u):
            g = groups[u["gi"]]
            gb = g["gb"]
            dv = g["dv"]
            pt = u["pt"]
            for i in range(2):
                kb = u["kb0"] + i
                for qb in range(4):
                    st_ = bool(u["first"] and i == 0 and qb % 2 == 0)
                    sp_ = bool(u["last"] and i == 1)
                    P.op("pe", lambda e, i=i, kb=kb, qb=qb, st_=st_, sp_=sp_: e.matmul(
                        O_ap(qb, dv), lhsT=pt.ap[:, i * 512 + qb * 128:i * 512 + (qb + 1) * 128], rhs=gb["V"][:, kb, 0:dv + 1],
                        start=st_, stop=sp_, skip_group_check=True),
                        reads=[pt.buf, gb["b_v"]], writes=[k.PB[OB[0]], k.PB[OB[1]]], mark=(i == 1 and qb == 3))
            if u["last"]:
                finalize(u, g, dv)
            if u["glast"] and u["gi"] + 2 < len(groups):
                load_group(u["gi"] + 2)

        def finalize(u, g, dv):
            j = u["j"]
            if g["kind"] == "d":
                osl = Oraws.items[u["mi"]]
            else:
                osl = Oraws.next()
            w2 = 2 * (dv + 1)
            P.op("dve", lambda e: e.tensor_copy(out=osl.ap[:, :, 0:w2], in_=k.ps[:, OB[0]:OB[1] + 1, 0:w2]),
                 reads=[k.PB[OB[0]], k.PB[OB[1]]], writes=[osl.buf])

            def Ov(sl, qb, lo, hi):
                base = (qb % 2) * (dv + 1)
                return sl.ap[:, qb // 2, base + lo:base + hi]
            for qb in range(4):
                P.op("dve", lambda e, qb=qb: e.reciprocal(out=fin[:, qb:qb + 1], in_=Ov(osl, qb, dv, dv + 1)), reads=[osl.buf], writes=[B_fin])
            if g["kind"] == "f":
                for qb in range(4):
                    P.op("dve", lambda e, qb=qb: e.tensor_scalar(out=Y[:, j * 4 + qb, g["ycol"]:g["ycol"] + 64], in0=Ov(osl, qb, 0, 64),
                                                                 scalar1=fin[:, qb:qb + 1], scalar2=None, op0=ALU.mult),
                         reads=[osl.buf, B_fin], writes=[B_Y])
                return
            if u["mi"] == 0:
                for qb in range(4):
                    P.op("dve", lambda e, qb=qb: e.tensor_scalar(out=Ov(osl, qb, 0, 128), in0=Ov(osl, qb, 0, 128), scalar1=fin[:, qb:qb + 1],
                                                                 scalar2=None, op0=ALU.mult), reads=[B_fin], writes=[osl.buf])
                return
            y1s = Oraws.items[0]
            P.op("dve", lambda e: e.tensor_scalar(out=fin[:, 4:8], in0=fin[:, 0:4], scalar1=neglam, scalar2=None, op0=ALU.mult),
                 reads=[B_fin, B_c], writes=[B_fin])
            for qb in range(4):
                P.op("dve", lambda e, qb=qb: e.scalar_tensor_tensor(out=Ov(osl, qb, 0, 128), in0=Ov(osl, qb, 0, 128), scalar=fin[:, 4 + qb:5 + qb],
                                                                    in1=Ov(y1s, qb, 0, 128), op0=ALU.mult, op1=ALU.add),
                     reads=[B_fin, y1s.buf], writes=[osl.buf])
            for qb in range(4):
                P.op("act", lambda e, qb=qb: e.activation(out=k.junk[:, 0:128], in_=Ov(osl, qb, 0, 128), func=AF.Square,
                                                          accum_out=fin[:, 8 + qb:9 + qb]), reads=[osl.buf], writes=[B_fin])
            P.op("act", lambda e: e.activation(out=fin[:, 8:12], in_=fin[:, 8:12], func=AF.Ln, scale=1.0 / 128, bias=k.epsb[:, 0:1]),
                 reads=[B_fin], writes=[B_fin])
            P.op("act", lambda e: e.activation(out=fin[:, 12:16], in_=fin[:, 8:12], func=AF.Exp, scale=-0.5), reads=[B_fin], writes=[B_fin])
            for qb in range(4):
                P.op("dve", lambda e, qb=qb: e.scalar_tensor_tensor(out=Y[:, j * 4 + qb, g["ycol"]:g["ycol"] + 128], in0=Ov(osl, qb, 0, 128),
                                                                    scalar=fin[:, 12 + qb:13 + qb], in1=g8[:], op0=ALU.mult, op1=ALU.mult),
                     reads=[osl.buf, B_fin, B_c], writes=[B_Y])

        LAG = 3
        load_group(0)
        if len(groups) > 1:
            load_group(1)
        n = len(units)
        for idx in range(n + LAG):
            if idx < n:
                emit_qk(units[idx])
            if idx >= LAG:
                emit_av(units[idx - LAG])
        P.barrier()


def phaseM(k, P, KmT, Vm, B_mem):
    nc = k.nc
    io = k.io
    with ExitStack() as st:
        sb = mk_sb(nc, st)
        Wkv = sb("Wkv", [128, 8, 2 * D], BF16)
        gkv = sb("gkv", [128, 8], F32)
        s_m = P.dsem("d_miscM")
        s_mx = [P.dsem("d_memx0"), P.dsem("d_memx1")]
        t_g = P.dma(gkv[:], io["g_mem_kvT"], s_m)
        wait_all(P, ["pool", "dve", "act"], [t_g])
        stg = Ring([Slot(sb("wst", [128, 1024], F32), Buf(), P.dsem(f"d_wstM{i}")) for i in range(6)])
        tl = []
        convert_weight(k, P, io["w_kv_mem"], 8, 2 * D, Wkv, gkv, stg, tl)
        wait_all(P, ["pe"], tl)
        memT = Slot(sb("memT", [128, 8, 256], BF16), Buf())
        smalls = sb("smM", [128, 4], F32)
        small = (smalls[:, 0:1], smalls[:, 1:2], smalls[:, 2:3], Buf())
        for b in range(2):
            xb = Slot(sb("memx", [128, D], F32), Buf(), s_mx[b])
            P.dma(xb.ap, io["mem"][b * 128:(b + 1) * 128, :], s_mx[b], writes=[xb.buf])
            hn = Slot(sb("memhn", [128, D], BF16), Buf())
            norm_block(k, P, xb.ap, xb.buf, hn, small, 1.0 / D, D)
            transpose_block(k, P, hn.ap, hn.buf, 8, b, memT.ap[:, :, b * 128:(b + 1) * 128], memT.buf, "act")
        P.op("dve", lambda e: e.memset(Vm[:, :, :, 256:257], 1.0), writes=[B_mem])
        ring = Ring([2, 3, 4, 5, 6, 7])
        for c in range(8):
            bank = ring.next()
            for kc in range(8):
                P.op("pe", lambda e, kc=kc, c=c, bank=bank: e.matmul(k.ps[:, bank, 0:256], lhsT=Wkv[:, kc, c * 128:(c + 1) * 128],
                                                                     rhs=memT.ap[:, kc, :], start=(kc == 0), stop=(kc == 7)),
                     reads=[memT.buf], writes=[k.PB[bank]], mark=(kc == 7))
            evac_copy(P, "dve", KmT[:, c, :], k.ps[:, bank, 0:256], [k.PB[bank]], [B_mem])
        for mb in range(2):
            for half in range(2):
                bank = ring.next()
                for kc in range(8):
                    P.op("pe", lambda e, kc=kc, mb=mb, half=half, bank=bank: e.matmul(
                        k.ps[:, bank, :], lhsT=memT.ap[:, kc, mb * 128:(mb + 1) * 128], rhs=Wkv[:, kc, D + half * 512:D + (half + 1) * 512],
                        start=(kc == 0), stop=(kc == 7)), reads=[memT.buf], writes=[k.PB[bank]], mark=(kc == 7))
                evac_copy(P, "dve", Vm[:, mb, 2 * half:2 * half + 2, 0:256], k.ps[:, bank, :].rearrange("p (h d) -> p h d", h=2),
                          [k.PB[bank]], [B_mem])
        P.barrier()


def phase3a(k, P, Y, B_Y, KmT, Vm, B_mem):
    nc = k.nc
    io = k.io
    NT = 16 if not k.limit3 else k.limit3
    with ExitStack() as st:
        sb = mk_sb(nc, st)
        Wd = sb("Wd", [128, 4, D], BF16)
        Wf = sb("Wf", [128, 4, D], BF16)
        Wo = sb("Wo", [128, 8, D], BF16)
        Wqm = sb("Wqm", [128, 8, D], BF16)
        Wom = sb("Wom", [128, 8, D], BF16)
        gq = sb("gq", [128, 8], F32)
        s_m = P.dsem("d_misc3a")
        t_g = P.dma(gq[:], io["g_mem_qT"], s_m)
        wait_all(P, ["pool", "dve", "act"], [t_g])
        stg = Ring([Slot(sb("wst", [128, 512], F32), Buf(), P.dsem(f"d_wst3a{i}")) for i in range(4)])
        def conv(name, nchunks, dst, gT):
            tl = []
            convert_weight(k, P, io[name], nchunks, D, dst, gT, stg, tl)
            wait_all(P, ["pe"], tl)
        P.wait("pe", B_mem.w)
        P.wait("pe", B_Y.w)

        xts = [Slot(sb("xt", [128, 2, D], F32), Buf(), P.dsem(f"d_xt3a{i}")) for i in range(2)]
        sgs = [Slot(sb("sg", [128, 8, 2, 256], BF16), Buf(), P.dsem(f"d_sg3a{i}")) for i in range(2)]
        sts = [P.dsem("d_x2store0"), P.dsem("d_x2store1")]
        yT = Slot(sb("yT", [128, 8, 256], BF16), Buf())
        tmps = Ring([Slot(sb("tmp", [128, 512], F32), Buf()) for _ in range(2)])
        mT = Slot(sb("mT", [128, 8, 256], BF16), Buf())
        hns = Ring([Slot(sb("hn3", [128, D], BF16), Buf()) for _ in range(2)])
        h2T = mT
        QmT = Slot(sb("QmT", [128, 8, 256], BF16), Buf())
        Pms = Ring([Slot(sb("Pm", [128, 512], BF16), Buf()) for _ in range(4)])
        om = Slot(sb("om", [128, 2, D], BF16), Buf())
        omT = yT
        smt = sb("sm3", [128, 16], F32)
        smalls = Ring([(smt[:, 4 * i:4 * i + 1], smt[:, 4 * i + 1:4 * i + 2], smt[:, 4 * i + 2:4 * i + 3], Buf()) for i in range(2)])
        rec = Ring([(smt[:, 8 + i:9 + i], Buf()) for i in range(4)])
        tb = Ring([0, 1])
        ring = Ring([2, 3, 4, 5, 6, 7])

        def load_tile(t):
            xt, sg = xts[t % 2], sgs[t % 2]
            P.dma(xt.ap, io["xq"][t * 256:(t + 1) * 256, :].rearrange("(b p) c -> p b c", p=128), xt.sem, writes=[xt.buf])
            P.dma(sg.ap, k.SGs[:, :, :, t * 256:(t + 1) * 256].rearrange("c a p t -> p c a t"), sg.sem, writes=[sg.buf])

        def tok_proj_add(srcT, W, xt):
            for blk in range(2):
                for half in range(2):
                    bank = ring.next()
                    for kk in range(8):
                        P.op("pe", lambda e, kk=kk, blk=blk, half=half, bank=bank: e.matmul(
                            k.ps[:, bank, :], lhsT=srcT.ap[:, kk, blk * 128:(blk + 1) * 128], rhs=W[:, kk, half * 512:(half + 1) * 512],
                            start=(kk == 0), stop=(kk == 7)), reads=[srcT.buf], writes=[k.PB[bank]], mark=(kk == 7))
                    dst = xt.ap[:, blk, half * 512:(half + 1) * 512]
                    P.op("dve", lambda e, dst=dst, bank=bank: e.tensor_tensor(out=dst, in0=k.ps[:, bank, :], in1=dst, op=ALU.add),
                         reads=[k.PB[bank], xt.buf], writes=[xt.buf])

        sgtmp = tmps

        def sigmoid_tile(t):
            sg_ = sgs[t % 2]
            for c in range(8):
                tmp = sgtmp.next()
                v = sg_.ap[:, c, :, :].rearrange("p a t -> p (a t)")
                P.op("act", lambda e, tmp=tmp, v=v: e.activation(out=tmp.ap, in_=v, func=AF.Exp, scale=-1.0), reads=[sg_.buf], writes=[tmp.buf])
                P.op("act", lambda e, tmp=tmp: e.activation(out=tmp.ap, in_=tmp.ap, func=AF.Ln, bias=k.oneb[:, 0:1]), reads=[tmp.buf], writes=[tmp.buf])
                P.op("act", lambda e, tmp=tmp, v=v: e.activation(out=v, in_=tmp.ap, func=AF.Exp, scale=-1.0), reads=[tmp.buf], writes=[sg_.buf])

        load_tile(0)
        sigmoid_tile(0)
        for t in range(NT):
            if t + 1 < NT:
                load_tile(t + 1)
            xt, sg = xts[t % 2], sgs[t % 2]
            if t == 0:
                conv("w_diff_out", 4, Wd, None)
                conv("w_fox_out", 4, Wf, None)
            for blk in range(2):
                transpose_block(k, P, Y[:, t * 2 + blk, :], B_Y, 8, tb.next(), yT.ap[:, :, blk * 128:(blk + 1) * 128], yT.buf, "dve")
            for c in range(8):
                bank = ring.next()
                for kk in range(4):
                    P.op("pe", lambda e, kk=kk, c=c, bank=bank: e.matmul(k.ps[:, bank, 0:256], lhsT=Wd[:, kk, c * 128:(c + 1) * 128],
                                                                         rhs=yT.ap[:, kk, :], start=(kk == 0), stop=(kk == 3)),
                         reads=[yT.buf], writes=[k.PB[bank]], mark=False)
                for kk in range(4):
                    P.op("pe", lambda e, kk=kk, c=c, bank=bank: e.matmul(k.ps[:, bank, 256:512], lhsT=Wf[:, kk, c * 128:(c + 1) * 128],
                                                                         rhs=yT.ap[:, 4 + kk, :], start=(kk == 0), stop=(kk == 3)),
                         reads=[yT.buf], writes=[k.PB[bank]], mark=(kk == 3))
                tmp = tmps.next()
                P.op("dve", lambda e, c=c, bank=bank, tmp=tmp, sg=sg: e.tensor_tensor(out=tmp.ap, in0=k.ps[:, bank, :],
                                                                               in1=sg.ap[:, c, :, :].rearrange("p a t -> p (a t)"), op=ALU.mult),
                     reads=[k.PB[bank], sg.buf], writes=[tmp.buf])
                P.op("pool", lambda e, c=c, tmp=tmp: e.tensor_tensor(out=mT.ap[:, c, :], in0=tmp.ap[:, 0:256], in1=tmp.ap[:, 256:512], op=ALU.add),
                     reads=[tmp.buf], writes=[mT.buf])
            if t == 0:
                conv("w_o", 8, Wo, None)
            tok_proj_add(mT, Wo, xt)
            if t == 0:
                conv("w_q_mem", 8, Wqm, gq)
            hl = []
            for blk in range(2):
                hn = hns.next()
                norm_block(k, P, xt.ap[:, blk, :], xt.buf, hn, smalls.next(), 1.0 / D, D)
                hl.append(hn)
            for blk, hn in enumerate(hl):
                transpose_block(k, P, hn.ap, hn.buf, 8, tb.next(), h2T.ap[:, :, blk * 128:(blk + 1) * 128], h2T.buf, "act" if blk == 0 else "dve")
            for c in range(8):
                bank = ring.next()
                for kk in range(8):
                    P.op("pe", lambda e, kk=kk, c=c, bank=bank: e.matmul(k.ps[:, bank, 0:256], lhsT=Wqm[:, kk, c * 128:(c + 1) * 128],
                                                                         rhs=h2T.ap[:, kk, :], start=(kk == 0), stop=(kk == 7)),
                         reads=[h2T.buf], writes=[k.PB[bank]], mark=(kk == 7))
                evac_copy(P, "dve" if c % 2 == 0 else "act", QmT.ap[:, c, :], k.ps[:, bank, 0:256], [k.PB[bank]], [QmT.buf], scale=1.0 / 16)
            sbanks_ = []
            for hm in range(4):
                bank = ring.next()
                sbanks_.append(bank)
                for mb in range(2):
                    for dc in range(2):
                        P.op("pe", lambda e, mb=mb, dc=dc, hm=hm, bank=bank: e.matmul(
                            k.ps[:, bank, mb * 256:(mb + 1) * 256], lhsT=KmT[:, 2 * hm + dc, mb * 128:(mb + 1) * 128], rhs=QmT.ap[:, 2 * hm + dc, :],
                            start=(dc == 0), stop=(dc == 1)), reads=[QmT.buf], writes=[k.PB[bank]], mark=(mb == 1 and dc == 1))
            pms_ = []
            for hm in range(4):
                bank = sbanks_[hm]
                pm = Pms.next()
                pms_.append(pm)
                P.op("act", lambda e, bank=bank, pm=pm: e.activation(out=pm.ap, in_=k.ps[:, bank, :], func=AF.Exp),
                     reads=[k.PB[bank]], writes=[pm.buf])
            for hm in range(4):
                pm = pms_[hm]
                for blk in range(2):
                    b2 = ring.next()
                    for mb in range(2):
                        P.op("pe", lambda e, mb=mb, blk=blk, hm=hm, b2=b2, pm=pm: e.matmul(
                            k.ps[:, b2, 0:257], lhsT=pm.ap[:, mb * 256 + blk * 128:mb * 256 + (blk + 1) * 128], rhs=Vm[:, mb, hm, 0:257],
                            start=(mb == 0), stop=(mb == 1)), reads=[pm.buf], writes=[k.PB[b2]], mark=(mb == 1))
                    rc, rb = rec.next()
                    P.op("dve", lambda e, b2=b2, rc=rc: e.reciprocal(out=rc, in_=k.ps[:, b2, 256:257]), reads=[k.PB[b2]], writes=[rb])
                    P.op("dve", lambda e, b2=b2, rc=rc, blk=blk, hm=hm: e.tensor_scalar(
                        out=om.ap[:, blk, hm * 256:(hm + 1) * 256], in0=k.ps[:, b2, 0:256], scalar1=rc, scalar2=None, op0=ALU.mult),
                        reads=[k.PB[b2], rb], writes=[om.buf])
            for blk in range(2):
                transpose_block(k, P, om.ap[:, blk, :], om.buf, 8, tb.next(), omT.ap[:, :, blk * 128:(blk + 1) * 128], omT.buf, "act" if blk == 0 else "dve")
            if t == 0:
                conv("w_o_mem", 8, Wom, None)
            if t + 1 < NT:
                sigmoid_tile(t + 1)
            tok_proj_add(omT, Wom, xt)
            P.dma(k.X2s[t * 256:(t + 1) * 256, :].rearrange("(b p) c -> p b c", p=128), xt.ap, sts[t % 2], reads=[xt.buf])
        P.barrier()


def phase3c(k, P):
    nc = k.nc
    io = k.io
    NT = 16 if not k.limit3 else k.limit3
    with ExitStack() as st:
        sb = mk_sb(nc, st)
        W1 = sb("W1", [128, 8, 4 * D], BF16)
        W2 = sb("W2", [128, 32, D], BF16)
        gm = sb("gm", [128, 8], F32)
        gfin = sb("gfin", [128, D], F32)
        s_m = P.dsem("d_misc3c")
        t_g = P.dma(gm[:], io["g_mlpT"], s_m)
        t_f = P.dma(gfin[:], io["g_final"].broadcast_to([128, D]), s_m)
        wait_all(P, ["pool", "dve", "act"], [t_g, t_f])
        xts = [Slot(sb("xt", [128, 2, D], F32), Buf(), P.dsem(f"d_xt3c{i}")) for i in range(3)]
        ots = [Slot(sb("ot", [128, 2, D], F32), Buf(), P.dsem(f"d_ot3c{i}")) for i in range(1)]
        hns = Ring([Slot(sb("hn4", [128, D], BF16), Buf()) for _ in range(2)])
        h3T = Slot(sb("h3T", [128, 8, 256], BF16), Buf())
        rs = Ring([Slot(sb("rr", [128, 512], BF16), Buf()) for _ in range(2)])
        aT = Slot(sb("aT", [128, 32, 256], BF16), Buf())
        smt = sb("sm4", [128, 16], F32)
        smalls = Ring([(smt[:, 4 * i:4 * i + 1], smt[:, 4 * i + 1:4 * i + 2], smt[:, 4 * i + 2:4 * i + 3], Buf()) for i in range(4)])
        tb = Ring([0, 1])
        ring = Ring([2, 3, 4, 5, 6, 7])

        def load_tile(t):
            xt = xts[t % 3]
            P.dma(xt.ap, k.X2s[t * 256:(t + 1) * 256, :].rearrange("(b p) c -> p b c", p=128), xt.sem, writes=[xt.buf])

        h3Ts = [h3T, Slot(sb("h3Tb", [128, 8, 256], BF16), Buf())]

        def emit_norm(t):
            xt_ = xts[t % 3]
            hl = []
            for blk in range(2):
                hn = hns.next()
                norm_block(k, P, xt_.ap[:, blk, :], xt_.buf, hn, smalls.next(), 1.0 / D, D)
                hl.append(hn)
            return hl

        def emit_T(t, hl):
            hT_ = h3Ts[t % 2]
            for blk, hn in enumerate(hl):
                transpose_block(k, P, hn.ap, hn.buf, 8, tb.next(), hT_.ap[:, :, blk * 128:(blk + 1) * 128], hT_.buf,
                                "act" if blk == 0 else "dve")

        load_tile(0)
        if NT > 1:
            load_tile(1)
        stg = Ring([Slot(sb("wst", [128, 512], F32), Buf(), P.dsem(f"d_wst3c{i}")) for i in range(4)])
        tl = []
        convert_weight(k, P, io["w1"], 8, 4 * D, W1, gm, stg, tl)
        convert_weight(k, P, io["w2"], 32, D, W2, None, stg, tl)
        wait_all(P, ["pe"], tl)

        hl_cur = emit_norm(0)
        emit_T(0, hl_cur)
        for t in range(NT):
            xt, ot = xts[t % 3], ots[0]
            h3T = h3Ts[t % 2]
            if t + 2 < NT:
                load_tile(t + 2)
            hl_next = emit_norm(t + 1) if t + 1 < NT else None
            for f2 in range(16):
                bank = ring.next()
                for i in range(2):
                    f = 2 * f2 + i
                    for kk in range(8):
                        P.op("pe", lambda e, kk=kk, f=f, i=i, bank=bank, h3T=h3T: e.matmul(k.ps[:, bank, i * 256:(i + 1) * 256], lhsT=W1[:, kk, f * 128:(f + 1) * 128],
                                                                                  rhs=h3T.ap[:, kk, :], start=(kk == 0), stop=(kk == 7)),
                             reads=[h3T.buf], writes=[k.PB[bank]], mark=(i == 1 and kk == 7))
                r = rs.next()
                P.op("act", lambda e, bank=bank, r=r: e.activation(out=r.ap, in_=k.ps[:, bank, :], func=AF.Relu), reads=[k.PB[bank]], writes=[r.buf])
                P.op("pool", lambda e, r=r, f2=f2: e.tensor_tensor(out=aT.ap[:, 2 * f2:2 * f2 + 2, :].rearrange("p a t -> p (a t)"), in0=r.ap, in1=r.ap,
                                                                   op=ALU.mult), reads=[r.buf], writes=[aT.buf])
            if hl_next is not None:
                emit_T(t + 1, hl_next)
            for blk in range(2):
                for half in range(2):
                    bank = ring.next()
                    for f in range(32):
                        P.op("pe", lambda e, f=f, blk=blk, half=half, bank=bank: e.matmul(
                            k.ps[:, bank, :], lhsT=aT.ap[:, f, blk * 128:(blk + 1) * 128], rhs=W2[:, f, half * 512:(half + 1) * 512],
                            start=(f == 0), stop=(f == 31)), reads=[aT.buf], writes=[k.PB[bank]], mark=(f == 31))
                    dst = xt.ap[:, blk, half * 512:(half + 1) * 512]
                    P.op("dve", lambda e, dst=dst, bank=bank: e.tensor_tensor(out=dst, in0=k.ps[:, bank, :], in1=dst, op=ALU.add),
                         reads=[k.PB[bank], xt.buf], writes=[xt.buf])
            for blk in range(2):
                ss, lnv, rstd, sbuf_ = smalls.next()
                xa = xt.ap[:, blk, :]
                P.op("act", lambda e, xa=xa, ss=ss, ot=ot, blk=blk: e.activation(out=ot.ap[:, blk, :], in_=xa, func=AF.Square, accum_out=ss),
                     reads=[xt.buf], writes=[sbuf_, ot.buf])
                P.op("act", lambda e, ss=ss, lnv=lnv: e.activation(out=lnv, in_=ss, func=AF.Ln, scale=1.0 / D, bias=k.epsb[:, 0:1]), reads=[sbuf_], writes=[sbuf_])
                P.op("act", lambda e, rstd=rstd, lnv=lnv: e.activation(out=rstd, in_=lnv, func=AF.Exp, scale=-0.5), reads=[sbuf_], writes=[sbuf_])
                P.op("dve", lambda e, xa=xa, rstd=rstd, blk=blk, ot=ot: e.scalar_tensor_tensor(out=ot.ap[:, blk, :], in0=xa, scalar=rstd, in1=gfin[:],
                                                                                         op0=ALU.mult, op1=ALU.mult),
                     reads=[xt.buf, sbuf_], writes=[ot.buf])
            P.dma(k.out[t * 256:(t + 1) * 256, :].rearrange("(b p) c -> p b c", p=128), ot.ap, ot.sem, reads=[ot.buf])
        P.barrier()

def build(stage, limit_tiles=0, limit_groups=None, limit_j=0, limit3=0):
    nc = bass.Bass("TRN2", target_bir_lowering=False)
    k = K()
    k.nc = nc
    k.limit_tiles = limit_tiles
    k.limit_groups = limit_groups
    k.limit_j = limit_j
    k.limit3 = limit3
    io = {}

    def din(name, shape, dt=F32):
        io[name] = nc.dram_tensor(name, list(shape), dt, kind="ExternalInput").ap()

    din("xkv", (S, D))
    din("xq", (NQ, D))
    din("mem", (256, D))
    din("w_in", (D, INC))
    din("b_forget", (8, 1))
    din("rflag", (8, 1))
    din("lam4", (4, 64))
    din("g_subln", (1, 128))
    din("Lraw", (5, 2, 128, LW))
    din("t31", (5, 1))
    din("w_diff_out", (512, D))
    din("w_fox_out", (512, D))
    din("w_o", (D, D))
    din("g_mixT", (128, 8))
    din("g_mem_qT", (128, 8))
    din("g_mem_kvT", (128, 8))
    din("w_q_mem", (D, D))
    din("w_kv_mem", (D, 2 * D))
    din("w_o_mem", (D, D))
    din("g_mlpT", (128, 8))
    din("w1", (D, 4 * D))
    din("w2", (4 * D, D))
    din("g_final", (1, D))
    din("ident", (128, 128))
    k.io = io
    dbg = stage < 9
    skind = "ExternalOutput" if dbg else "Internal"
    k.KTd = nc.dram_tensor("KTd", [4, 128, S], BF16, kind=skind).ap()
    k.QTd = nc.dram_tensor("QTd", [4, 128, NQ], BF16, kind=skind).ap()
    k.KTf = nc.dram_tensor("KTf", [8, 68, S], BF16, kind=skind).ap()
    k.QTf = nc.dram_tensor("QTf", [8, 68, NQ], BF16, kind=skind).ap()
    k.Vs = nc.dram_tensor("Vs", [S, D], BF16, kind=skind).ap()
    k.SGs = nc.dram_tensor("SGs", [8, 2, 128, NQ], BF16, kind=skind).ap()
    k.out = nc.dram_tensor("out", [NQ, D], F32, kind="ExternalOutput").ap()
    k.X2s = nc.dram_tensor("X2s", [NQ, D], F32, kind=skind).ap()
    if stage == 2:
        k.Ydbg = nc.dram_tensor("Ydbg", [128, 32, D], BF16, kind="ExternalOutput").ap()

    with ExitStack() as st:
        P = Prog(nc, st)
        k.P = P
        k.ps = st.enter_context(nc.psum_tensor("ps", [128, 8, 512], F32))
        k.PB = [Buf(f"ps{i}") for i in range(8)]
        sb = mk_sb(nc, st)
        idf = sb("idf", [128, 128], F32)
        k.idb = sb("idb", [128, 128], BF16)
        k.epsb = sb("epsb", [128, 1], F32)
        k.oneb = sb("oneb", [128, 1], F32)
        k.junk = sb("junk", [128, 128], BF16)
        s0 = P.dsem("d_const")
        t = P.dma(idf[:], io["ident"], s0)
        P.wait("dve", t)
        t1 = P.op("dve", lambda e: e.tensor_copy(out=k.idb[:], in_=idf[:]))
        t2 = P.op("pool", lambda e: e.memset(k.epsb[:], EPS))
        t2 = P.op("pool", lambda e: e.memset(k.oneb[:], 1.0))
        wait_all(P, ["pe", "act", "dve"], [t1, t2])

        phase1(k, P)
        if stage >= 2:
            with ExitStack() as stY:
                Y = stY.enter_context(nc.sbuf_tensor("Yres", [128, 32, D], BF16))
                B_Y = Buf()
                phase2(k, P, Y, B_Y)
                if stage == 2:
                    sY = P.dsem("d_ydbg")
                    P.dma(k.Ydbg, Y[:], sY, reads=[B_Y])
                if stage >= 3:
                    KmT = stY.enter_context(nc.sbuf_tensor("KmT", [128, 8, 256], BF16))
                    Vm = stY.enter_context(nc.sbuf_tensor("Vm", [128, 2, 4, 258], BF16))
                    B_mem = Buf()
                    phaseM(k, P, KmT, Vm, B_mem)
                    phase3a(k, P, Y, B_Y, KmT, Vm, B_mem)
            if stage >= 3:
                phase3c(k, P)
        for s in P.dsems:
            if s.val:
                P.wait("sp", (s, s.val))
        P.emit()
    return nc


_NC_CACHE = {}


def t5_bucket_np(dist):
    dist = np.maximum(dist, 0)
    max_exact = 16
    d = np.maximum(dist, 1).astype(np.float32)
    large = max_exact + (np.log(d / np.float32(max_exact)) / np.float32(math.log(128 / max_exact)) * np.float32(32 - max_exact)).astype(np.int32)
    large = np.minimum(large, 31)
    return np.where(dist < max_exact, dist, large)


def build_Lraw(rel_bias, r):
    L = np.empty((5, 2, 128, LW), np.float32)
    kk = np.arange(128)[:, None]
    qq = np.arange(128)[None, :]
    for par in range(2):
        off = 4 if (r == 0) == (par == 0) else 8
        for u in range(15):
            delta = u - 11 + off
            dist = delta * 128 + qq - kk
            bucket = t5_bucket_np(dist)
            for h in range(5):
                if h < 4:
                    tile = rel_bias[bucket, h]
                else:
                    tile = np.zeros((128, 128), np.float32)
                tile = np.where(dist < 0, np.float32(MASKV), tile)
                L[h, par, :, u * 128:(u + 1) * 128] = tile
    return L


def make_inputs(c, x, mem, w_in, b_forget, lambda_q1, lambda_k1, lambda_q2, lambda_k2, g_subln, rel_bias,
                w_diff_out, w_fox_out, w_o, g_mix, g_mem_q, g_mem_kv, w_q_mem, w_kv_mem, w_o_mem, g_mlp, w1, w2, g_final):
    b, r = c // 2, c % 2
    f = lambda a: np.ascontiguousarray(np.asarray(a, dtype=np.float32))
    xb = f(x[b])
    tiles = own_tiles(r)
    xq = np.concatenate([xb[t * 512:(t + 1) * 512] for t in tiles], axis=0)
    gT = lambda g: f(np.asarray(g[0]).reshape(8, 128).T)
    t31 = np.zeros((5, 1), np.float32)
    t31[0:4, 0] = np.asarray(rel_bias)[31, :]
    return {
        "xkv": xb, "xq": f(xq), "mem": f(mem[b]), "w_in": f(w_in[0]),
        "b_forget": f(np.asarray(b_forget[0]).reshape(8, 1)),
        "rflag": np.full((8, 1), float(r), np.float32),
        "lam4": f(np.stack([lambda_q1[0], lambda_k1[0], lambda_q2[0], lambda_k2[0]])),
        "g_subln": f(np.asarray(g_subln[0]).reshape(1, 128)),
        "Lraw": build_Lraw(np.asarray(rel_bias, dtype=np.float32), r), "t31": t31,
        "w_diff_out": f(w_diff_out[0]), "w_fox_out": f(w_fox_out[0]), "w_o": f(w_o[0]),
        "g_mixT": gT(g_mix), "g_mem_qT": gT(g_mem_q), "g_mem_kvT": gT(g_mem_kv),
        "w_q_mem": f(w_q_mem[0]), "w_kv_mem": f(w_kv_mem[0]), "w_o_mem": f(w_o_mem[0]),
        "g_mlpT": gT(g_mlp), "w1": f(w1[0]), "w2": f(w2[0]),
        "g_final": f(np.asarray(g_final).reshape(1, D)),
        "ident": np.eye(128, dtype=np.float32),
    }


def kernel(**inputs):
    inputs = {k_: np.asarray(v) for k_, v in inputs.items()}
    if "nc" not in _NC_CACHE:
        _NC_CACHE["nc"] = build(9)
    nc = _NC_CACHE["nc"]
    in_maps = [make_inputs(c, **inputs) for c in range(8)]
    res = run_bass_kernel_spmd(nc, in_maps, core_ids=list(range(8)))
    out = np.empty((4, S, D), np.float32)
    for c in range(8):
        b, r = c // 2, c % 2
        o = np.asarray(res.results[c]["out"], dtype=np.float32)
        for j, t in enumerate(own_tiles(r)):
            out[b, t * 512:(t + 1) * 512] = o[j * 512:(j + 1) * 512]
    return out
```

```python
import math
from contextlib import ExitStack

import numpy as np
import concourse.bass as bass
import concourse.mybir as mybir
from concourse.bass_utils import run_bass_kernel_spmd

F32 = mybir.dt.float32
BF16 = mybir.dt.bfloat16
AF = mybir.ActivationFunctionType
ALU = mybir.AluOpType

S = 8192
D = 1024
NQ = 4096
INC = 5128
EPS = 1e-6
MASKV = -30000.0
LW = 15 * 128

O_QA, O_KA, O_VA, O_QB, O_KB, O_VB, O_FL, O_GA, O_GB = 0, 512, 1024, 1536, 2048, 2560, 3072, 3080, 4104


def own_tiles(r):
    out = []
    for j in range(8):
        g = j // 2
        if r == 0:
            out.append(4 * g + (0 if j % 2 == 0 else 3))
        else:
            out.append(4 * g + (1 if j % 2 == 0 else 2))
    return out


def nk_tiles(j):
    g = j // 2
    return 4 * g + 2 if j % 2 == 0 else 4 * g + 4


class Buf:
    __slots__ = ("w", "r", "name")

    def __init__(self, name=""):
        self.w = None
        self.r = {}
        self.name = name


class DSem:
    __slots__ = ("h", "val", "name")

    def __init__(self, h, name):
        self.h = h
        self.val = 0
        self.name = name


class Prog:
    ENG = ("sp", "pe", "act", "dve", "pool")

    def __init__(self, nc, stack):
        self.nc = nc
        self.stack = stack
        self.q = {e: [] for e in self.ENG}
        self.sem = {e: stack.enter_context(nc.semaphore("s_" + e)) for e in ("pe", "act", "dve", "pool")}
        self.cnt = {e: 0 for e in self.ENG}
        self.seen = {e: {} for e in self.ENG}
        self.dsems = []

    def dsem(self, name):
        s = DSem(self.stack.enter_context(self.nc.semaphore(name)), name)
        self.dsems.append(s)
        return s

    def wait(self, eng, t):
        key, val = t
        if key == eng and (eng == "pe" or val > self.cnt[eng]):
            return
        if self.seen[eng].get(key, 0) >= val:
            return
        self.seen[eng][key] = val
        self.q[eng].append(("wait", key, val))

    def _deps(self, eng, reads, writes):
        for b in reads:
            if b.w is not None:
                self.wait(eng, b.w)
        for b in writes:
            if b.w is not None:
                self.wait(eng, b.w)
            for k, v in b.r.items():
                self.wait(eng, (k, v))

    @staticmethod
    def _upd(t, reads, writes):
        k, v = t
        for b in reads:
            if b.r.get(k, 0) < v:
                b.r[k] = v
        for b in writes:
            b.w = t
            b.r = {}

    def op(self, eng, fn, reads=(), writes=(), mark=True):
        self._deps(eng, reads, writes)
        if mark:
            self.cnt[eng] += 1
            t = (eng, self.cnt[eng])
        else:
            t = (eng, self.cnt[eng] + 1)
        self.q[eng].append(("op", fn, mark))
        self._upd(t, reads, writes)
        return t

    def dma(self, out_ap, in_ap, sem, reads=(), writes=(), q="sp"):
        self._deps(q, reads, writes)
        sem.val += 16
        t = (sem, sem.val)
        self.q[q].append(("dma", out_ap, in_ap, sem))
        self._upd(t, reads, writes)
        return t

    def dma_multi(self, pairs, sem, reads=(), writes=(), q="sp"):
        self._deps(q, reads, writes)
        for o, i in pairs:
            sem.val += 16
            self.q[q].append(("dma", o, i, sem))
        t = (sem, sem.val)
        self._upd(t, reads, writes)
        return t

    def barrier(self):
        ts = [(e, self.cnt[e]) for e in ("pe", "act", "dve", "pool") if self.cnt[e] > 0]
        ts += [(s, s.val) for s in self.dsems if s.val > 0]
        for e in self.ENG:
            for t in ts:
                self.wait(e, t)

    def emit(self):
        nc = self.nc
        for e in self.ENG:
            for it in self.q[e]:
                if it[0] == "wait" and isinstance(it[1], str):
                    assert it[2] <= self.cnt[it[1]], ("unreachable wait", e, it[1], it[2], self.cnt[it[1]])
        with nc.allow_low_precision("bf16 matmul operands / fp32 accumulation by design"), nc.Block() as block:
            decos = {"sp": block.sync, "pe": block.tensor, "act": block.scalar,
                     "dve": block.vector, "pool": block.gpsimd}
            for e in self.ENG:
                items = self.q[e]

                def body(engine, items=items, e=e):
                    for it in items:
                        if it[0] == "wait":
                            key = it[1]
                            h = self.sem[key] if isinstance(key, str) else key.h
                            engine.wait_ge(h, it[2])
                        elif it[0] == "op":
                            ins = it[1](engine)
                            if it[2]:
                                ins.then_inc(self.sem[e], 1)
                        else:
                            _, o, i, s = it
                            engine.dma_start(out=o, in_=i).then_inc(s.h, 16)

                decos[e](body)


class Slot:
    __slots__ = ("ap", "buf", "sem")

    def __init__(self, ap, buf, sem=None):
        self.ap = ap[:]
        self.buf = buf
        self.sem = sem


class Ring:
    def __init__(self, items):
        self.items = items
        self.i = 0

    def next(self):
        it = self.items[self.i % len(self.items)]
        self.i += 1
        return it


class K:
    pass


_SB_CNT = [0]


def mk_sb(nc, st):
    cnt = _SB_CNT

    def sb(name, shape, dt):
        cnt[0] += 1
        return st.enter_context(nc.sbuf_tensor(f"{name}_{cnt[0]}", shape, dt))
    return sb


def evac_copy(P, eng, out_ap, in_ap, reads, writes, scale=None):
    if eng == "act":
        if scale is None:
            return P.op("act", lambda e: e.copy(out=out_ap, in_=in_ap), reads=reads, writes=writes)
        return P.op("act", lambda e: e.mul(out=out_ap, in_=in_ap, mul=scale), reads=reads, writes=writes)
    if scale is None:
        return P.op(eng, lambda e: e.tensor_copy(out=out_ap, in_=in_ap), reads=reads, writes=writes)
    return P.op(eng, lambda e: e.tensor_scalar(out=out_ap, in0=in_ap, scalar1=scale, scalar2=None, op0=ALU.mult),
                reads=reads, writes=writes)


def norm_block(k, P, x_ap, x_buf, hn, small, inv_n, width):
    ss, lnv, rstd, sbuf_ = small
    P.op("act", lambda e: e.activation(out=hn.ap, in_=x_ap, func=AF.Square, accum_out=ss),
         reads=[x_buf], writes=[sbuf_, hn.buf])
    P.op("act", lambda e: e.activation(out=lnv, in_=ss, func=AF.Ln, scale=inv_n, bias=k.epsb[:, 0:1]),
         reads=[sbuf_], writes=[sbuf_])
    P.op("act", lambda e: e.activation(out=rstd, in_=lnv, func=AF.Exp, scale=-0.5), reads=[sbuf_], writes=[sbuf_])
    P.op("act", lambda e: e.activation(out=hn.ap, in_=x_ap, func=AF.Copy, scale=rstd), reads=[x_buf, sbuf_], writes=[hn.buf])


def transpose_block(k, P, src_ap, src_buf, nchunk, bank, dst_ap, dst_buf, eng):
    psb = k.ps[:, bank, :].bitcast(BF16)
    for c in range(nchunk):
        P.op("pe", lambda e, c=c: e.transpose(out=psb[:, c * 128:(c + 1) * 128], in_=src_ap[:, c * 128:(c + 1) * 128],
                                              identity=k.idb[:]),
             reads=[src_buf], writes=[k.PB[bank]], mark=(c == nchunk - 1))
    src = psb[:, 0:nchunk * 128].rearrange("p (c t) -> p c t", c=nchunk)
    evac_copy(P, eng, dst_ap, src, [k.PB[bank]], [dst_buf])


def convert_weight(k, P, w_dram, nrow_chunks, ncols, dst, gT, stg_ring, t_list, engines=("dve", "act")):
    i = 0
    pw = stg_ring.items[0].ap.shape[-1]
    for kc in range(nrow_chunks):
        for c0 in range(0, ncols, pw):
            cw = min(pw, ncols - c0)
            sg = stg_ring.next()
            P.dma(sg.ap[:, 0:cw], w_dram[kc * 128:(kc + 1) * 128, c0:c0 + cw], sg.sem, writes=[sg.buf])
            i += 1
            dst_ap = dst[:, kc, c0:c0 + cw]
            src_ap = sg.ap[:, 0:cw]
            if gT is None:
                if engines[i % len(engines)] == "dve":
                    t = P.op("dve", lambda e, d=dst_ap, s=src_ap: e.tensor_copy(out=d, in_=s), reads=[sg.buf])
                else:
                    t = P.op("act", lambda e, d=dst_ap, s=src_ap: e.copy(out=d, in_=s), reads=[sg.buf])
            else:
                gs = gT[:, kc:kc + 1]
                t = P.op("act", lambda e, d=dst_ap, s=src_ap, gs=gs: e.activation(out=d, in_=s, func=AF.Copy, scale=gs), reads=[sg.buf])
            t_list.append(t)


def wait_all(P, engs, tickets):
    for e in engs:
        for t in tickets:
            P.wait(e, t)


def phase1(k, P):
    nc = k.nc
    io = k.io
    with ExitStack() as st_outer:
      flT = mk_sb(nc, st_outer)("flT", [8, S], F32)
      B_fl = Buf()
      with ExitStack() as st:
        sb = mk_sb(nc, st)
        winb = sb("winb", [128, 8, INC], BF16)
        gmix = sb("gmix", [128, 8], F32)
        s_misc = P.dsem("d_misc1")
        t_g = P.dma(gmix[:], io["g_mixT"], s_misc)
        wait_all(P, ["pool", "dve", "act"], [t_g])
        stg = Ring([Slot(sb("wst", [128, 512], F32), Buf(), P.dsem(f"d_wst{i}")) for i in range(4)])
        stgp = Ring([Slot(sb("wstp", [128, 512], F32), Buf(), P.dsem(f"d_wstp{i}")) for i in range(2)])
        slab_t = {}

        def convert_slab(sl_i, eng):
            c0 = sl_i * 512
            cw = min(512, INC - c0)
            ts = []
            for kc in range(8):
                if eng == "act":
                    sg = stg.next()
                    P.dma(sg.ap[:, 0:cw], io["w_in"][kc * 128:(kc + 1) * 128, c0:c0 + cw], sg.sem, writes=[sg.buf])
                    ts.append(P.op("act", lambda e, sg=sg, kc=kc: e.activation(out=winb[:, kc, c0:c0 + cw], in_=sg.ap[:, 0:cw], func=AF.Copy,
                                                                               scale=gmix[:, kc:kc + 1]), reads=[sg.buf]))
                else:
                    sg = stgp.next()
                    P.dma(sg.ap[:, 0:cw], io["w_in"][kc * 128:(kc + 1) * 128, c0:c0 + cw], sg.sem, writes=[sg.buf], q="pool")
                    ts.append(P.op("pool", lambda e, sg=sg, kc=kc: e.tensor_scalar(out=winb[:, kc, c0:c0 + cw], in0=sg.ap[:, 0:cw],
                                                                                   scalar1=gmix[:, kc:kc + 1], scalar2=None, op0=ALU.mult),
                                   reads=[sg.buf]))
            slab_t[sl_i] = ts

        def need_cols(col0, ncol):
            for sl_i in range(col0 // 512, (col0 + ncol - 1) // 512 + 1):
                for t in slab_t[sl_i]:
                    P.wait("pe", t)

        xblk = Ring([Slot(sb("xblk", [128, 1024], F32), Buf(), P.dsem(f"d_x{i}")) for i in range(3)])
        hns = Ring([Slot(sb("hn", [128, 1024], BF16), Buf()) for _ in range(4)])
        smalls = []
        for i in range(4):
            t_ = sb("sm", [128, 4], F32)
            smalls.append((t_[:, 0:1], t_[:, 1:2], t_[:, 2:3], Buf()))
        smalls = Ring(smalls)
        hTs = Ring([Slot(sb("hT", [128, 8, 512], BF16), Buf()) for _ in range(2)])
        stKd = Slot(sb("stKd", [128, 4, 512], BF16), Buf(), P.dsem("d_stKd"))
        stKf = Slot(sb("stKf", [128, 4, 512], BF16), Buf(), P.dsem("d_stKf"))
        stV = Slot(sb("stV", [128, 4, 1024], BF16), Buf(), P.dsem("d_stV"))
        stQd = Slot(sb("stQd", [128, 4, 512], BF16), Buf(), P.dsem("d_stQd"))
        stQf = Slot(sb("stQf", [128, 4, 512], BF16), Buf(), P.dsem("d_stQf"))
        stSG = Slot(sb("stSG", [128, 8, 2, 512], BF16), Buf(), P.dsem("d_stSG"))
        tb_ring = Ring([0, 1])
        pj_ring = Ring([2, 3, 4, 5, 6, 7])

        blk_list = []
        pending = []

        def issue_load():
            if blk_list:
                src_dram, r0 = blk_list.pop(0)
                sl = xblk.next()
                P.dma(sl.ap[:], src_dram[r0:r0 + 128, :], sl.sem, writes=[sl.buf])
                pending.append(sl)

        def make_hT():
            hT = hTs.next()
            hl = []
            for b in range(4):
                sl = pending.pop(0)
                hn = hns.next()
                norm_block(k, P, sl.ap[:], sl.buf, hn, smalls.next(), 1.0 / D, D)
                issue_load()
                hl.append(hn)
            for b, hn in enumerate(hl):
                transpose_block(k, P, hn.ap, hn.buf, 8, tb_ring.next(), hT.ap[:, :, b * 128:(b + 1) * 128], hT.buf,
                                "act" if b % 2 == 0 else "dve")
            return hT

        def proj_feat(hT, col0, ncol, ntok=512):
            bank = pj_ring.next()
            need_cols(col0, ncol)
            for kc in range(8):
                P.op("pe", lambda e, kc=kc: e.matmul(k.ps[0:ncol, bank, 0:ntok], lhsT=winb[:, kc, col0:col0 + ncol],
                                                     rhs=hT.ap[:, kc, 0:ntok], start=(kc == 0), stop=(kc == 7)),
                     reads=[hT.buf], writes=[k.PB[bank]], mark=(kc == 7))
            return bank

        def proj_tok(hT, blk, col0):
            bank = pj_ring.next()
            need_cols(col0, 512)
            for kc in range(8):
                P.op("pe", lambda e, kc=kc: e.matmul(k.ps[:, bank, :], lhsT=hT.ap[:, kc, blk * 128:(blk + 1) * 128],
                                                     rhs=winb[:, kc, col0:col0 + 512], start=(kc == 0), stop=(kc == 7)),
                     reads=[hT.buf], writes=[k.PB[bank]], mark=(kc == 7))
            return bank

        def kv_tile(t, hT):
            c0 = t * 512
            for h in range(4):
                bank = proj_feat(hT, O_KA + h * 128, 128)
                evac_copy(P, "dve", stKd.ap[:, h, :], k.ps[:, bank, :], [k.PB[bank]], [stKd.buf])
            P.dma(k.KTd[:, :, c0:c0 + 512].rearrange("h p t -> p h t"), stKd.ap[:], stKd.sem, reads=[stKd.buf])
            for h in range(4):
                bank = proj_feat(hT, O_KB + h * 128, 128)
                evac_copy(P, "dve", stKf.ap[:, h, :], k.ps[:, bank, :], [k.PB[bank]], [stKf.buf])
            for half in range(2):
                P.dma(k.KTf[half::2, 0:64, c0:c0 + 512].rearrange("h p t -> p h t"), stKf.ap[half * 64:(half + 1) * 64, :, :],
                      stKf.sem, reads=[stKf.buf])
            bank = proj_feat(hT, O_FL, 8)
            evac_copy(P, "dve", flT[:, c0:c0 + 512], k.ps[0:8, bank, :], [k.PB[bank]], [B_fl])
            for blk in range(4):
                for half, col0 in enumerate((O_VA, O_VB)):
                    bank = proj_tok(hT, blk, col0)
                    evac_copy(P, "dve" if half == 0 else "act", stV.ap[:, blk, half * 512:(half + 1) * 512], k.ps[:, bank, :],
                              [k.PB[bank]], [stV.buf])
            P.dma(k.Vs[c0:c0 + 512, :].rearrange("(b p) c -> p b c", p=128), stV.ap[:], stV.sem, reads=[stV.buf])

        def q_tile(j, hT):
            c0 = j * 512
            for h in range(4):
                bank = proj_feat(hT, O_QA + h * 128, 128)
                evac_copy(P, "dve", stQd.ap[:, h, :], k.ps[:, bank, :], [k.PB[bank]], [stQd.buf], scale=0.125)
            P.dma(k.QTd[:, :, c0:c0 + 512].rearrange("h p t -> p h t"), stQd.ap[:], stQd.sem, reads=[stQd.buf])
            for h in range(4):
                bank = proj_feat(hT, O_QB + h * 128, 128)
                evac_copy(P, "dve", stQf.ap[:, h, :], k.ps[:, bank, :], [k.PB[bank]], [stQf.buf], scale=0.125)
            for half in range(2):
                P.dma(k.QTf[half::2, 0:64, c0:c0 + 512].rearrange("h p t -> p h t"), stQf.ap[half * 64:(half + 1) * 64, :, :],
                      stQf.sem, reads=[stQf.buf])
            for ab, col0 in enumerate((O_GA, O_GB)):
                for c in range(8):
                    bank = proj_feat(hT, col0 + c * 128, 128)
                    evac_copy(P, "dve" if c % 2 == 0 else "act", stSG.ap[:, c, ab, :], k.ps[:, bank, :], [k.PB[bank]], [stSG.buf])
            P.dma(k.SGs[:, :, :, c0:c0 + 512].rearrange("c a p t -> p c a t"), stSG.ap[:], stSG.sem, reads=[stSG.buf])

        order = [("kv", t) for t in range(16)] + [("q", j) for j in range(8)]
        if k.limit_tiles:
            order = order[:k.limit_tiles]
        for kind, idx in order:
            for b in range(4):
                blk_list.append((io["xkv"] if kind == "kv" else io["xq"], idx * 512 + b * 128))
        for _ in range(3):
            issue_load()
        hT_next = make_hT()
        for sl_i in (1, 4, 6, 2, 5):
            convert_slab(sl_i, "act")
        for sl_i in (0, 3, 7, 8, 9, 10):
            convert_slab(sl_i, "pool")
        for oi, (kind, idx) in enumerate(order):
            hT_cur = hT_next
            if oi + 1 < len(order):
                hT_next = make_hT()
            if kind == "kv":
                kv_tile(idx, hT_cur)
            else:
                q_tile(idx, hT_cur)

        P.barrier()
      with ExitStack() as st2:
        if not k.limit_tiles:
            fox_rows(k, P, mk_sb(nc, st2), flT, B_fl)
        P.barrier()


def fox_rows(k, P, sb, flT, B_fl):
    io = k.io
    s_m = P.dsem("d_fox")
    bf = sb("bfg", [8, 2], F32)
    rfl = sb("rfl", [8, 1], F32)
    B_s = Buf()
    s_ld = P.dsem("d_fox_ld")
    s_one = P.dsem("d_fox_one")
    P.dma_multi([(bf[:, 0:1], io["b_forget"]), (rfl[:], io["rflag"])], s_ld, writes=[B_s])
    P.op("dve", lambda e: e.tensor_scalar(out=bf[:, 1:2], in0=bf[:, 0:1], scalar1=-1.0, scalar2=None, op0=ALU.mult),
         reads=[B_s], writes=[B_s])
    ones = sb("ones", [8, 2048], F32)
    onesb = sb("onesb", [8, 4096], BF16)
    B_o = Buf()
    P.op("pool", lambda e: e.memset(ones[:], 1.0), writes=[B_o])
    P.op("pool", lambda e: e.memset(onesb[:], 1.0), writes=[B_o])
    one1 = sb("one1", [8, 1], F32)
    P.op("pool", lambda e: e.memset(one1[:], 1.0), writes=[B_o])
    Chi = sb("Chi", [8, S], BF16)
    BCh = [Buf() for _ in range(S // 2048)]
    CW = 2048
    NCH = S // CW
    tA = [sb("ctA", [8, CW], F32) for _ in range(NCH)]
    tB = [sb("ctB", [8, CW], F32) for _ in range(NCH)]
    tC = [sb("ctC", [8, CW], F32) for _ in range(NCH)]
    cmid = [sb("cmid", [8, CW], BF16) for _ in range(NCH)]
    clo = [sb("clo", [8, CW], BF16) for _ in range(NCH)]
    BA = [Buf() for _ in range(NCH)]
    BB = [Buf() for _ in range(NCH)]
    for ci in range(NCH):
        c0 = ci * CW
        P.op("act", lambda e, c0=c0, ci=ci: e.activation(out=tA[ci][:], in_=flT[:, c0:c0 + CW], func=AF.Exp, scale=-1.0, bias=bf[:, 1:2]),
             reads=[B_fl, B_s], writes=[BA[ci]])
        P.op("act", lambda e, ci=ci: e.activation(out=tA[ci][:], in_=tA[ci][:], func=AF.Ln, bias=one1[:, 0:1]), reads=[BA[ci], B_o], writes=[BA[ci]])
    for ci in range(NCH):
        c0 = ci * CW
        init = 0.0 if ci == 0 else flT[:, c0 - 1:c0]
        P.op("dve", lambda e, c0=c0, init=init, ci=ci: e.tensor_tensor_scan(out=flT[:, c0:c0 + CW], data0=ones[:], data1=tA[ci][:], initial=init,
                                                                         op0=ALU.mult, op1=ALU.add),
             reads=[BA[ci], B_o, B_fl], writes=[B_fl])
        P.op("dve", lambda e, c0=c0: e.tensor_copy(out=Chi[:, c0:c0 + CW], in_=flT[:, c0:c0 + CW]), reads=[B_fl], writes=[BCh[ci]])
        P.op("dve", lambda e, c0=c0, ci=ci: e.tensor_tensor(out=tB[ci][:], in0=flT[:, c0:c0 + CW], in1=Chi[:, c0:c0 + CW], op=ALU.subtract),
             reads=[B_fl, BCh[ci]], writes=[BB[ci]])
        P.op("dve", lambda e, ci=ci: e.tensor_copy(out=cmid[ci][:], in_=tB[ci][:]), reads=[BB[ci]], writes=[BB[ci]])
        P.op("dve", lambda e, ci=ci: e.tensor_tensor(out=tC[ci][:], in0=tB[ci][:], in1=cmid[ci][:], op=ALU.subtract), reads=[BB[ci]], writes=[BB[ci]])
        P.op("dve", lambda e, ci=ci: e.tensor_copy(out=clo[ci][:], in_=tC[ci][:]), reads=[BB[ci]], writes=[BB[ci]])
        P.dma(k.KTf[:, 64, c0:c0 + CW], onesb[:, 0:CW], s_one, reads=[B_o])
        P.dma(k.KTf[:, 65, c0:c0 + CW], Chi[:, c0:c0 + CW], s_m, reads=[BCh[ci]])
        P.dma(k.KTf[:, 66, c0:c0 + CW], cmid[ci][:], s_m, reads=[BB[ci]])
        P.dma(k.KTf[:, 67, c0:c0 + CW], clo[ci][:], s_m, reads=[BB[ci]])
    qrow = sb("qrow", [8, NQ], BF16)
    dtmp = sb("dtmp", [8, 512], F32)
    B_q = Buf()
    ta, tb = own_tiles(0), own_tiles(1)
    for j in range(8):
        a0, b0 = ta[j] * 512, tb[j] * 512
        P.op("dve", lambda e, a0=a0, b0=b0: e.tensor_tensor(out=dtmp[:], in0=Chi[:, b0:b0 + 512], in1=Chi[:, a0:a0 + 512], op=ALU.subtract),
             reads=BCh, writes=[B_q])
        P.op("dve", lambda e, a0=a0: e.scalar_tensor_tensor(out=dtmp[:], in0=dtmp[:], scalar=rfl[:, 0:1], in1=Chi[:, a0:a0 + 512],
                                                            op0=ALU.mult, op1=ALU.add), reads=[B_q, B_s] + BCh, writes=[B_q])
        P.op("dve", lambda e, j=j: e.tensor_scalar(out=qrow[:, j * 512:(j + 1) * 512], in0=dtmp[:], scalar1=-1.0, scalar2=None, op0=ALU.mult),
             reads=[B_q], writes=[B_q])
    P.dma(k.QTf[:, 64, :], qrow[:], s_m, reads=[B_q])
    for rr in (65, 66, 67):
        P.dma(k.QTf[:, rr, :], onesb[:], s_one, reads=[B_o])


def phase2(k, P, Y, B_Y):
    nc = k.nc
    io = k.io
    with ExitStack() as st:
        sb = mk_sb(nc, st)
        G = []
        for i in range(2):
            G.append(dict(KT=sb("KTt", [128, S], BF16)[:], QA=sb("QA", [128, NQ], BF16)[:], QB=sb("QB", [128, NQ], BF16)[:],
                          V=sb("Vt", [128, 64, 130], BF16)[:], L=sb("Lt", [128, 2, LW], F32)[:], t31=sb("t31t", [128, 1], F32)[:],
                          b_k=Buf(), b_q=Buf(), b_v=Buf(), b_l=Buf(), sem=P.dsem(f"d_grp{i}")))
        for gb_ in (G if k.limit_groups else []):
            for nm, bn in (("KT", "b_k"), ("QA", "b_q"), ("QB", "b_q")):
                P.op("dve", lambda e, gb_=gb_, nm=nm: e.memset(gb_[nm], 0.0), writes=[gb_[bn]])
        zerob = sb("zerob", [128, 1], F32)
        tz = P.op("dve", lambda e: e.memset(zerob[:], 0.0))
        P.wait("act", tz)
        Pts = Ring([Slot(sb("Pt", [128, 1024], BF16), Buf()) for _ in range(4)])
        sbanks = Ring([0, 2, 4])
        OB = (6, 7)
        s_m = P.dsem("d_misc2")
        lamt = sb("lamt", [128, 256], F32)
        g8 = sb("g8", [128, 128], F32)
        sm = sb("sm2", [128, 8], F32)
        B_c = Buf()
        P.dma(lamt[:], io["lam4"].rearrange("(o a) d -> o (a d)", o=1).broadcast_to([128, 256]), s_m, writes=[B_c])
        P.dma(g8[:], io["g_subln"].broadcast_to([128, 128]), s_m, writes=[B_c])
        P.op("dve", lambda e: e.tensor_scalar(out=g8[:], in0=g8[:], scalar1=0.8, scalar2=None, op0=ALU.mult), reads=[B_c], writes=[B_c])
        P.op("dve", lambda e: e.tensor_tensor(out=lamt[:, 0:64], in0=lamt[:, 0:64], in1=lamt[:, 64:128], op=ALU.mult), reads=[B_c], writes=[B_c])
        P.op("dve", lambda e: e.tensor_tensor(out=lamt[:, 128:192], in0=lamt[:, 128:192], in1=lamt[:, 192:256], op=ALU.mult), reads=[B_c], writes=[B_c])
        P.op("dve", lambda e: e.reduce_sum(out=sm[:, 0:1], in_=lamt[:, 0:64], axis=mybir.AxisListType.X), reads=[B_c], writes=[B_c])
        P.op("dve", lambda e: e.reduce_sum(out=sm[:, 1:2], in_=lamt[:, 128:192], axis=mybir.AxisListType.X), reads=[B_c], writes=[B_c])
        P.op("act", lambda e: e.activation(out=sm[:, 2:4], in_=sm[:, 0:2], func=AF.Exp), reads=[B_c], writes=[B_c])
        P.op("dve", lambda e: e.tensor_tensor(out=sm[:, 4:5], in0=sm[:, 3:4], in1=sm[:, 2:3], op=ALU.subtract), reads=[B_c], writes=[B_c])
        P.op("dve", lambda e: e.tensor_scalar(out=sm[:, 4:5], in0=sm[:, 4:5], scalar1=-0.2, scalar2=None, op0=ALU.add), reads=[B_c], writes=[B_c])
        neglam = sm[:, 4:5]
        fin = sb("fin", [128, 16], F32)
        B_y1, B_oa, B_fin = Buf(), Buf(), Buf()
        Oraws = Ring([Slot(sb("Oraw", [128, 2, 258], F32), Buf()) for _ in range(2)])

        groups = []
        for h in range(4):
            groups.append(dict(kind="d", h=h, dv=128, lidx=h, ycol=h * 128))
        for h in range(8):
            groups.append(dict(kind="f", h=h, dv=64, lidx=4, ycol=512 + h * 64))
        if k.limit_groups:
            groups = [groups[i] for i in k.limit_groups]

        def load_group(gi):
            g = groups[gi]
            gb = G[gi % 2]
            g["gb"] = gb
            dv = g["dv"]
            h = g["h"]
            P.op("pool", lambda e: e.memset(gb["QA"][64:128, :], 0.0), writes=[gb["b_q"]])
            if g["kind"] == "d":
                P.op("pool", lambda e: e.memset(gb["QB"][0:64, :], 0.0), writes=[gb["b_q"]])
            P.op("pool", lambda e: e.memset(gb["V"][:, :, dv:dv + 1], 1.0), writes=[gb["b_v"]])
            if g["kind"] == "d":
                P.dma(gb["KT"][:, :], k.KTd[h], gb["sem"], writes=[gb["b_k"]])
                P.dma_multi([(gb["QA"][0:64, :], k.QTd[h, 0:64, :]), (gb["QB"][64:128, :], k.QTd[h, 64:128, :])], gb["sem"], writes=[gb["b_q"]])
                vsrc = k.Vs[:, h * 128:(h + 1) * 128]
            else:
                P.dma(gb["KT"][0:68, :], k.KTf[h], gb["sem"], writes=[gb["b_k"]])
                P.dma(gb["QA"][0:68, :], k.QTf[h], gb["sem"], writes=[gb["b_q"]])
                vsrc = k.Vs[:, 512 + h * 64:512 + (h + 1) * 64]
            vsrc = vsrc.rearrange("(kb p) d -> p kb d", p=128)
            P.dma_multi([(gb["V"][:, q8 * 8:(q8 + 1) * 8, 0:dv], vsrc[:, q8 * 8:(q8 + 1) * 8, :]) for q8 in range(8)], gb["sem"], writes=[gb["b_v"]])
            P.dma_multi([(gb["L"][:, :, :], io["Lraw"][g["lidx"]].rearrange("a p w -> p a w")),
                         (gb["t31"][:, :], io["t31"][g["lidx"]:g["lidx"] + 1, :].broadcast_to([128, 1]))], gb["sem"], writes=[gb["b_l"]])
            fin_t = (gb["sem"], gb["sem"].val)
            for bn in ("b_k", "b_q", "b_v", "b_l"):
                gb[bn].w = fin_t

        units = []
        for gi, g in enumerate(groups):
            nmaps = 2 if g["kind"] == "d" else 1
            for j in range(8 if not k.limit_j else k.limit_j):
                nk = nk_tiles(j)
                for mi in range(nmaps):
                    n_u = nk * 2
                    for ui in range(n_u):
                        kt, half = ui // 2, ui % 2
                        nmask = 3 if g["kind"] == "d" else 2
                        m0 = (kt - (nk - 3)) * 4 + half * 2 if kt >= nk - nmask else None
                        units.append(dict(gi=gi, j=j, mi=mi, kb0=kt * 4 + half * 2, m0=m0, first=(ui == 0), last=(ui == n_u - 1),
                                          glast=(ui == n_u - 1 and mi == nmaps - 1 and j == (7 if not k.limit_j else k.limit_j - 1))))

        def O_ap(qb, dv):
            return k.ps[:, OB[qb // 2], (qb % 2) * (dv + 1):(qb % 2 + 1) * (dv + 1)]

        def emit_qk(u):
            g = groups[u["gi"]]
            gb = g["gb"]
            sb0 = sbanks.next()
            u["sb0"] = sb0
            qt = gb["QA"] if u["mi"] == 0 else gb["QB"]
            j = u["j"]
            for i in range(2):
                kb = u["kb0"] + i
                P.op("pe", lambda e, i=i, kb=kb: e.matmul(k.ps[:, sb0 + i, :], lhsT=gb["KT"][:, kb * 128:(kb + 1) * 128],
                                                         rhs=qt[:, j * 512:(j + 1) * 512], start=True, stop=True),
                     reads=[gb["b_k"], gb["b_q"]], writes=[k.PB[sb0 + i]], mark=(i == 1))
            if u["m0"] is not None:
                par = j % 2
                for i in range(2):
                    m = u["m0"] + i
                    off = (11 - m) * 128
                    P.op("dve", lambda e, i=i, off=off: e.tensor_tensor(out=k.ps[:, sb0 + i, :], in0=k.ps[:, sb0 + i, :],
                                                                        in1=gb["L"][:, par, off:off + 512], op=ALU.add),
                         reads=[gb["b_l"]], writes=[k.PB[sb0 + i]])
            pt = Pts.next()
            u["pt"] = pt
            bias_ap = zerob[:, 0:1] if u["m0"] is not None else gb["t31"][:, 0:1]
            P.op("act", lambda e: e.activation(out=pt.ap.rearrange("p (a b) -> p a b", a=2), in_=k.ps[:, sb0:sb0 + 2, :], func=AF.Exp,
                                               bias=bias_ap),
                 reads=[k.PB[sb0], k.PB[sb0 + 1], gb["b_l"]], writes=[pt.buf])

        def emit_av(u):
            g = groups[u["gi"]]
            gb = g["gb"]
            dv = g["dv"]
            pt = u["pt"]
            for i in range(2):
                kb = u["kb0"] + i
                for qb in range(4):
                    st_ = bool(u["first"] and i == 0 and qb % 2 == 0)
                    sp_ = bool(u["last"] and i == 1)
                    P.op("pe", lambda e, i=i, kb=kb, qb=qb, st_=st_, sp_=sp_: e.matmul(
                        O_ap(qb, dv), lhsT=pt.ap[:, i * 512 + qb * 128:i * 512 + (qb + 1) * 128], rhs=gb["V"][:, kb, 0:dv + 1],
                        start=st_, stop=sp_, skip_group_check=True),
                        reads=[pt.buf, gb["b_v"]], writes=[k.PB[OB[0]], k.PB[OB[1]]], mark=(i == 1 and qb == 3))
            if u["last"]:
                finalize(u, g, dv)
            if u["glast"] and u["gi"] + 2 < len(groups):
                load_group(u["gi"] + 2)

        def finalize(u, g, dv):
            j = u["j"]
            if g["kind"] == "d":
                osl = Oraws.items[u["mi"]]
            else:
                osl = Oraws.next()
            w2 = 2 * (dv + 1)
            P.op("dve", lambda e: e.tensor_copy(out=osl.ap[:, :, 0:w2], in_=k.ps[:, OB[0]:OB[1] + 1, 0:w2]),
                 reads=[k.PB[OB[0]], k.PB[OB[1]]], writes=[osl.buf])

            def Ov(sl, qb, lo, hi):
                base = (qb % 2) * (dv + 1)
                return sl.ap[:, qb // 2, base + lo:base + hi]
            for qb in range(4):
                P.op("dve", lambda e, qb=qb: e.reciprocal(out=fin[:, qb:qb + 1], in_=Ov(osl, qb, dv, dv + 1)), reads=[osl.buf], writes=[B_fin])
            if g["kind"] == "f":
                for qb in range(4):
                    P.op("dve", lambda e, qb=qb: e.tensor_scalar(out=Y[:, j * 4 + qb, g["ycol"]:g["ycol"] + 64], in0=Ov(osl, qb, 0, 64),
                                                                 scalar1=fin[:, qb:qb + 1], scalar2=None, op0=ALU.mult),
                         reads=[osl.buf, B_fin], writes=[B_Y])
                return
            if u["mi"] == 0:
                for qb in range(4):
                    P.op("dve", lambda e, qb=qb: e.tensor_scalar(out=Ov(osl, qb, 0, 128), in0=Ov(osl, qb, 0, 128), scalar1=fin[:, qb:qb + 1],
                                                                 scalar2=None, op0=ALU.mult), reads=[B_fin], writes=[osl.buf])
                return
            y1s = Oraws.items[0]
            P.op("dve", lambda e: e.tensor_scalar(out=fin[:, 4:8], in0=fin[:, 0:4], scalar1=neglam, scalar2=None, op0=ALU.mult),
                 reads=[B_fin, B_c], writes=[B_fin])
            for qb in range(4):
                P.op("dve", lambda e, qb=qb: e.scalar_tensor_tensor(out=Ov(osl, qb, 0, 128), in0=Ov(osl, qb, 0, 128), scalar=fin[:, 4 + qb:5 + qb],
                                                                    in1=Ov(y1s, qb, 0, 128), op0=ALU.mult, op1=ALU.add),
                     reads=[B_fin, y1s.buf], writes=[osl.buf])
            for qb in range(4):
                P.op("act", lambda e, qb=qb: e.activation(out=k.junk[:, 0:128], in_=Ov(osl, qb, 0, 128), func=AF.Square,
                                                          accum_out=fin[:, 8 + qb:9 + qb]), reads=[osl.buf], writes=[B_fin])
            P.op("act", lambda e: e.activation(out=fin[:, 8:12], in_=fin[:, 8:12], func=AF.Ln, scale=1.0 / 128, bias=k.epsb[:, 0:1]),
                 reads=[B_fin], writes=[B_fin])
            P.op("act", lambda e: e.activation(out=fin[:, 12:16], in_=fin[:, 8:12], func=AF.Exp, scale=-0.5), reads=[B_fin], writes=[B_fin])
            for qb in range(4):
                P.op("dve", lambda e, qb=qb: e.scalar_tensor_tensor(out=Y[:, j * 4 + qb, g["ycol"]:g["ycol"] + 128], in0=Ov(osl, qb, 0, 128),
                                                                    scalar=fin[:, 12 + qb:13 + qb], in1=g8[:], op0=ALU.mult, op1=ALU.mult),
                     reads=[osl.buf, B_fin, B_c], writes=[B_Y])

        LAG = 3
        load_group(0)
        if len(groups) > 1:
            load_group(1)
        n = len(units)
        for idx in range(n + LAG):
            if idx < n:
                emit_qk(units[idx])
            if idx >= LAG:
                emit_av(units[idx - LAG])
        P.barrier()


def phaseM(k, P, KmT, Vm, B_mem):
    nc = k.nc
    io = k.io
    with ExitStack() as st:
        sb = mk_sb(nc, st)
        Wkv = sb("Wkv", [128, 8, 2 * D], BF16)
        gkv = sb("gkv", [128, 8], F32)
        s_m = P.dsem("d_miscM")
        s_mx = [P.dsem("d_memx0"), P.dsem("d_memx1")]
        t_g = P.dma(gkv[:], io["g_mem_kvT"], s_m)
        wait_all(P, ["pool", "dve", "act"], [t_g])
        stg = Ring([Slot(sb("wst", [128, 1024], F32), Buf(), P.dsem(f"d_wstM{i}")) for i in range(6)])
        tl = []
        convert_weight(k, P, io["w_kv_mem"], 8, 2 * D, Wkv, gkv, stg, tl)
        wait_all(P, ["pe"], tl)
        memT = Slot(sb("memT", [128, 8, 256], BF16), Buf())
        smalls = sb("smM", [128, 4], F32)
        small = (smalls[:, 0:1], smalls[:, 1:2], smalls[:, 2:3], Buf())
        for b in range(2):
            xb = Slot(sb("memx", [128, D], F32), Buf(), s_mx[b])
            P.dma(xb.ap, io["mem"][b * 128:(b + 1) * 128, :], s_mx[b], writes=[xb.buf])
            hn = Slot(sb("memhn", [128, D], BF16), Buf())
            norm_block(k, P, xb.ap, xb.buf, hn, small, 1.0 / D, D)
            transpose_block(k, P, hn.ap, hn.buf, 8, b, memT.ap[:, :, b * 128:(b + 1) * 128], memT.buf, "act")
        P.op("dve", lambda e: e.memset(Vm[:, :, :, 256:257], 1.0), writes=[B_mem])
        ring = Ring([2, 3, 4, 5, 6, 7])
        for c in range(8):
            bank = ring.next()
            for kc in range(8):
                P.op("pe", lambda e, kc=kc, c=c, bank=bank: e.matmul(k.ps[:, bank, 0:256], lhsT=Wkv[:, kc, c * 128:(c + 1) * 128],
                                                                     rhs=memT.ap[:, kc, :], start=(kc == 0), stop=(kc == 7)),
                     reads=[memT.buf], writes=[k.PB[bank]], mark=(kc == 7))
            evac_copy(P, "dve", KmT[:, c, :], k.ps[:, bank, 0:256], [k.PB[bank]], [B_mem])
        for mb in range(2):
            for half in range(2):
                bank = ring.next()
                for kc in range(8):
                    P.op("pe", lambda e, kc=kc, mb=mb, half=half, bank=bank: e.matmul(
                        k.ps[:, bank, :], lhsT=memT.ap[:, kc, mb * 128:(mb + 1) * 128], rhs=Wkv[:, kc, D + half * 512:D + (half + 1) * 512],
                        start=(kc == 0), stop=(kc == 7)), reads=[memT.buf], writes=[k.PB[bank]], mark=(kc == 7))
                evac_copy(P, "dve", Vm[:, mb, 2 * half:2 * half + 2, 0:256], k.ps[:, bank, :].rearrange("p (h d) -> p h d", h=2),
                          [k.PB[bank]], [B_mem])
        P.barrier()


def phase3a(k, P, Y, B_Y, KmT, Vm, B_mem):
    nc = k.nc
    io = k.io
    NT = 16 if not k.limit3 else k.limit3
    with ExitStack() as st:
        sb = mk_sb(nc, st)
        Wd = sb("Wd", [128, 4, D], BF16)
        Wf = sb("Wf", [128, 4, D], BF16)
        Wo = sb("Wo", [128, 8, D], BF16)
        Wqm = sb("Wqm", [128, 8, D], BF16)
        Wom = sb("Wom", [128, 8, D], BF16)
        gq = sb("gq", [128, 8], F32)
        s_m = P.dsem("d_misc3a")
        t_g = P.dma(gq[:], io["g_mem_qT"], s_m)
        wait_all(P, ["pool", "dve", "act"], [t_g])
        stg = Ring([Slot(sb("wst", [128, 512], F32), Buf(), P.dsem(f"d_wst3a{i}")) for i in range(4)])
        def conv(name, nchunks, dst, gT):
            tl = []
            convert_weight(k, P, io[name], nchunks, D, dst, gT, stg, tl)
            wait_all(P, ["pe"], tl)
        P.wait("pe", B_mem.w)
        P.wait("pe", B_Y.w)

        xts = [Slot(sb("xt", [128, 2, D], F32), Buf(), P.dsem(f"d_xt3a{i}")) for i in range(2)]
        sgs = [Slot(sb("sg", [128, 8, 2, 256], BF16), Buf(), P.dsem(f"d_sg3a{i}")) for i in range(2)]
        sts = [P.dsem("d_x2store0"), P.dsem("d_x2store1")]
        yT = Slot(sb("yT", [128, 8, 256], BF16), Buf())
        tmps = Ring([Slot(sb("tmp", [128, 512], F32), Buf()) for _ in range(2)])
        mT = Slot(sb("mT", [128, 8, 256], BF16), Buf())
        hns = Ring([Slot(sb("hn3", [128, D], BF16), Buf()) for _ in range(2)])
        h2T = mT
        QmT = Slot(sb("QmT", [128, 8, 256], BF16), Buf())
        Pms = Ring([Slot(sb("Pm", [128, 512], BF16), Buf()) for _ in range(4)])
        om = Slot(sb("om", [128, 2, D], BF16), Buf())
        omT = yT
        smt = sb("sm3", [128, 16], F32)
        smalls = Ring([(smt[:, 4 * i:4 * i + 1], smt[:, 4 * i + 1:4 * i + 2], smt[:, 4 * i + 2:4 * i + 3], Buf()) for i in range(2)])
        rec = Ring([(smt[:, 8 + i:9 + i], Buf()) for i in range(4)])
        tb = Ring([0, 1])
        ring = Ring([2, 3, 4, 5, 6, 7])

        def load_tile(t):
            xt, sg = xts[t % 2], sgs[t % 2]
            P.dma(xt.ap, io["xq"][t * 256:(t + 1) * 256, :].rearrange("(b p) c -> p b c", p=128), xt.sem, writes=[xt.buf])
            P.dma(sg.ap, k.SGs[:, :, :, t * 256:(t + 1) * 256].rearrange("c a p t -> p c a t"), sg.sem, writes=[sg.buf])

        def tok_proj_add(srcT, W, xt):
            for blk in range(2):
                for half in range(2):
                    bank = ring.next()
                    for kk in range(8):
                        P.op("pe", lambda e, kk=kk, blk=blk, half=half, bank=bank: e.matmul(
                            k.ps[:, bank, :], lhsT=srcT.ap[:, kk, blk * 128:(blk + 1) * 128], rhs=W[:, kk, half * 512:(half + 1) * 512],
                            start=(kk == 0), stop=(kk == 7)), reads=[srcT.buf], writes=[k.PB[bank]], mark=(kk == 7))
                    dst = xt.ap[:, blk, half * 512:(half + 1) * 512]
                    P.op("dve", lambda e, dst=dst, bank=bank: e.tensor_tensor(out=dst, in0=k.ps[:, bank, :], in1=dst, op=ALU.add),
                         reads=[k.PB[bank], xt.buf], writes=[xt.buf])

        sgtmp = tmps

        def sigmoid_tile(t):
            sg_ = sgs[t % 2]
            for c in range(8):
                tmp = sgtmp.next()
                v = sg_.ap[:, c, :, :].rearrange("p a t -> p (a t)")
                P.op("act", lambda e, tmp=tmp, v=v: e.activation(out=tmp.ap, in_=v, func=AF.Exp, scale=-1.0), reads=[sg_.buf], writes=[tmp.buf])
                P.op("act", lambda e, tmp=tmp: e.activation(out=tmp.ap, in_=tmp.ap, func=AF.Ln, bias=k.oneb[:, 0:1]), reads=[tmp.buf], writes=[tmp.buf])
                P.op("act", lambda e, tmp=tmp, v=v: e.activation(out=v, in_=tmp.ap, func=AF.Exp, scale=-1.0), reads=[tmp.buf], writes=[sg_.buf])

        load_tile(0)
        sigmoid_tile(0)
        for t in range(NT):
            if t + 1 < NT:
                load_tile(t + 1)
            xt, sg = xts[t % 2], sgs[t % 2]
            if t == 0:
                conv("w_diff_out", 4, Wd, None)
                conv("w_fox_out", 4, Wf, None)
            for blk in range(2):
                transpose_block(k, P, Y[:, t * 2 + blk, :], B_Y, 8, tb.next(), yT.ap[:, :, blk * 128:(blk + 1) * 128], yT.buf, "dve")
            for c in range(8):
                bank = ring.next()
                for kk in range(4):
                    P.op("pe", lambda e, kk=kk, c=c, bank=bank: e.matmul(k.ps[:, bank, 0:256], lhsT=Wd[:, kk, c * 128:(c + 1) * 128],
                                                                         rhs=yT.ap[:, kk, :], start=(kk == 0), stop=(kk == 3)),
                         reads=[yT.buf], writes=[k.PB[bank]], mark=False)
                for kk in range(4):
                    P.op("pe", lambda e, kk=kk, c=c, bank=bank: e.matmul(k.ps[:, bank, 256:512], lhsT=Wf[:, kk, c * 128:(c + 1) * 128],
                                                                         rhs=yT.ap[:, 4 + kk, :], start=(kk == 0), stop=(kk == 3)),
                         reads=[yT.buf], writes=[k.PB[bank]], mark=(kk == 3))
                tmp = tmps.next()
                P.op("dve", lambda e, c=c, bank=bank, tmp=tmp, sg=sg: e.tensor_tensor(out=tmp.ap, in0=k.ps[:, bank, :],
                                                                               in1=sg.ap[:, c, :, :].rearrange("p a t -> p (a t)"), op=ALU.mult),
                     reads=[k.PB[bank], sg.buf], writes=[tmp.buf])
                P.op("pool", lambda e, c=c, tmp=tmp: e.tensor_tensor(out=mT.ap[:, c, :], in0=tmp.ap[:, 0:256], in1=tmp.ap[:, 256:512], op=ALU.add),
                     reads=[tmp.buf], writes=[mT.buf])
            if t == 0:
                conv("w_o", 8, Wo, None)
            tok_proj_add(mT, Wo, xt)
            if t == 0:
                conv("w_q_mem", 8, Wqm, gq)
            hl = []
            for blk in range(2):
                hn = hns.next()
                norm_block(k, P, xt.ap[:, blk, :], xt.buf, hn, smalls.next(), 1.0 / D, D)
                hl.append(hn)
            for blk, hn in enumerate(hl):
                transpose_block(k, P, hn.ap, hn.buf, 8, tb.next(), h2T.ap[:, :, blk * 128:(blk + 1) * 128], h2T.buf, "act" if blk == 0 else "dve")
            for c in range(8):
                bank = ring.next()
                for kk in range(8):
                    P.op("pe", lambda e, kk=kk, c=c, bank=bank: e.matmul(k.ps[:, bank, 0:256], lhsT=Wqm[:, kk, c * 128:(c + 1) * 128],
                                                                         rhs=h2T.ap[:, kk, :], start=(kk == 0), stop=(kk == 7)),
                         reads=[h2T.buf], writes=[k.PB[bank]], mark=(kk == 7))
                evac_copy(P, "dve" if c % 2 == 0 else "act", QmT.ap[:, c, :], k.ps[:, bank, 0:256], [k.PB[bank]], [QmT.buf], scale=1.0 / 16)
            sbanks_ = []
            for hm in range(4):
                bank = ring.next()
                sbanks_.append(bank)
                for mb in range(2):
                    for dc in range(2):
                        P.op("pe", lambda e, mb=mb, dc=dc, hm=hm, bank=bank: e.matmul(
                            k.ps[:, bank, mb * 256:(mb + 1) * 256], lhsT=KmT[:, 2 * hm + dc, mb * 128:(mb + 1) * 128], rhs=QmT.ap[:, 2 * hm + dc, :],
                            start=(dc == 0), stop=(dc == 1)), reads=[QmT.buf], writes=[k.PB[bank]], mark=(mb == 1 and dc == 1))
            pms_ = []
            for hm in range(4):
                bank = sbanks_[hm]
                pm = Pms.next()
                pms_.append(pm)
                P.op("act", lambda e, bank=bank, pm=pm: e.activation(out=pm.ap, in_=k.ps[:, bank, :], func=AF.Exp),
                     reads=[k.PB[bank]], writes=[pm.buf])
            for hm in range(4):
                pm = pms_[hm]
                for blk in range(2):
                    b2 = ring.next()
                    for mb in range(2):
                        P.op("pe", lambda e, mb=mb, blk=blk, hm=hm, b2=b2, pm=pm: e.matmul(
                            k.ps[:, b2, 0:257], lhsT=pm.ap[:, mb * 256 + blk * 128:mb * 256 + (blk + 1) * 128], rhs=Vm[:, mb, hm, 0:257],
                            start=(mb == 0), stop=(mb == 1)), reads=[pm.buf], writes=[k.PB[b2]], mark=(mb == 1))
                    rc, rb = rec.next()
                    P.op("dve", lambda e, b2=b2, rc=rc: e.reciprocal(out=rc, in_=k.ps[:, b2, 256:257]), reads=[k.PB[b2]], writes=[rb])
                    P.op("dve", lambda e, b2=b2, rc=rc, blk=blk, hm=hm: e.tensor_scalar(
                        out=om.ap[:, blk, hm * 256:(hm + 1) * 256], in0=k.ps[:, b2, 0:256], scalar1=rc, scalar2=None, op0=ALU.mult),
                        reads=[k.PB[b2], rb], writes=[om.buf])
            for blk in range(2):
                transpose_block(k, P, om.ap[:, blk, :], om.buf, 8, tb.next(), omT.ap[:, :, blk * 128:(blk + 1) * 128], omT.buf, "act" if blk == 0 else "dve")
            if t == 0:
                conv("w_o_mem", 8, Wom, None)
            if t + 1 < NT:
                sigmoid_tile(t + 1)
            tok_proj_add(omT, Wom, xt)
            P.dma(k.X2s[t * 256:(t + 1) * 256, :].rearrange("(b p) c -> p b c", p=128), xt.ap, sts[t % 2], reads=[xt.buf])
        P.barrier()


def phase3c(k, P):
    nc = k.nc
    io = k.io
    NT = 16 if not k.limit3 else k.limit3
    with ExitStack() as st:
        sb = mk_sb(nc, st)
        W1 = sb("W1", [128, 8, 4 * D], BF16)
        W2 = sb("W2", [128, 32, D], BF16)
        gm = sb("gm", [128, 8], F32)
        gfin = sb("gfin", [128, D], F32)
        s_m = P.dsem("d_misc3c")
        t_g = P.dma(gm[:], io["g_mlpT"], s_m)
        t_f = P.dma(gfin[:], io["g_final"].broadcast_to([128, D]), s_m)
        wait_all(P, ["pool", "dve", "act"], [t_g, t_f])
        xts = [Slot(sb("xt", [128, 2, D], F32), Buf(), P.dsem(f"d_xt3c{i}")) for i in range(3)]
        ots = [Slot(sb("ot", [128, 2, D], F32), Buf(), P.dsem(f"d_ot3c{i}")) for i in range(1)]
        hns = Ring([Slot(sb("hn4", [128, D], BF16), Buf()) for _ in range(2)])
        h3T = Slot(sb("h3T", [128, 8, 256], BF16), Buf())
        rs = Ring([Slot(sb("rr", [128, 512], BF16), Buf()) for _ in range(2)])
        aT = Slot(sb("aT", [128, 32, 256], BF16), Buf())
        smt = sb("sm4", [128, 16], F32)
        smalls = Ring([(smt[:, 4 * i:4 * i + 1], smt[:, 4 * i + 1:4 * i + 2], smt[:, 4 * i + 2:4 * i + 3], Buf()) for i in range(4)])
        tb = Ring([0, 1])
        ring = Ring([2, 3, 4, 5, 6, 7])

        def load_tile(t):
            xt = xts[t % 3]
            P.dma(xt.ap, k.X2s[t * 256:(t + 1) * 256, :].rearrange("(b p) c -> p b c", p=128), xt.sem, writes=[xt.buf])

        h3Ts = [h3T, Slot(sb("h3Tb", [128, 8, 256], BF16), Buf())]

        def emit_norm(t):
            xt_ = xts[t % 3]
            hl = []
            for blk in range(2):
                hn = hns.next()
                norm_block(k, P, xt_.ap[:, blk, :], xt_.buf, hn, smalls.next(), 1.0 / D, D)
                hl.append(hn)
            return hl

        def emit_T(t, hl):
            hT_ = h3Ts[t % 2]
            for blk, hn in enumerate(hl):
                transpose_block(k, P, hn.ap, hn.buf, 8, tb.next(), hT_.ap[:, :, blk * 128:(blk + 1) * 128], hT_.buf,
                                "act" if blk == 0 else "dve")

        load_tile(0)
        if NT > 1:
            load_tile(1)
        stg = Ring([Slot(sb("wst", [128, 512], F32), Buf(), P.dsem(f"d_wst3c{i}")) for i in range(4)])
        tl = []
        convert_weight(k, P, io["w1"], 8, 4 * D, W1, gm, stg, tl)
        convert_weight(k, P, io["w2"], 32, D, W2, None, stg, tl)
        wait_all(P, ["pe"], tl)

        hl_cur = emit_norm(0)
        emit_T(0, hl_cur)
        for t in range(NT):
            xt, ot = xts[t % 3], ots[0]
            h3T = h3Ts[t % 2]
            if t + 2 < NT:
                load_tile(t + 2)
            hl_next = emit_norm(t + 1) if t + 1 < NT else None
            for f2 in range(16):
                bank = ring.next()
                for i in range(2):
                    f = 2 * f2 + i
                    for kk in range(8):
                        P.op("pe", lambda e, kk=kk, f=f, i=i, bank=bank, h3T=h3T: e.matmul(k.ps[:, bank, i * 256:(i + 1) * 256], lhsT=W1[:, kk, f * 128:(f + 1) * 128],
                                                                                  rhs=h3T.ap[:, kk, :], start=(kk == 0), stop=(kk == 7)),
                             reads=[h3T.buf], writes=[k.PB[bank]], mark=(i == 1 and kk == 7))
                r = rs.next()
                P.op("act", lambda e, bank=bank, r=r: e.activation(out=r.ap, in_=k.ps[:, bank, :], func=AF.Relu), reads=[k.PB[bank]], writes=[r.buf])
                P.op("pool", lambda e, r=r, f2=f2: e.tensor_tensor(out=aT.ap[:, 2 * f2:2 * f2 + 2, :].rearrange("p a t -> p (a t)"), in0=r.ap, in1=r.ap,
                                                                   op=ALU.mult), reads=[r.buf], writes=[aT.buf])
            if hl_next is not None:
                emit_T(t + 1, hl_next)
            for blk in range(2):
                for half in range(2):
                    bank = ring.next()
                    for f in range(32):
                        P.op("pe", lambda e, f=f, blk=blk, half=half, bank=bank: e.matmul(
                            k.ps[:, bank, :], lhsT=aT.ap[:, f, blk * 128:(blk + 1) * 128], rhs=W2[:, f, half * 512:(half + 1) * 512],
                            start=(f == 0), stop=(f == 31)), reads=[aT.buf], writes=[k.PB[bank]], mark=(f == 31))
                    dst = xt.ap[:, blk, half * 512:(half + 1) * 512]
                    P.op("dve", lambda e, dst=dst, bank=bank: e.tensor_tensor(out=dst, in0=k.ps[:, bank, :], in1=dst, op=ALU.add),
                         reads=[k.PB[bank], xt.buf], writes=[xt.buf])
            for blk in range(2):
                ss, lnv, rstd, sbuf_ = smalls.next()
                xa = xt.ap[:, blk, :]
                P.op("act", lambda e, xa=xa, ss=ss, ot=ot, blk=blk: e.activation(out=ot.ap[:, blk, :], in_=xa, func=AF.Square, accum_out=ss),
                     reads=[xt.buf], writes=[sbuf_, ot.buf])
                P.op("act", lambda e, ss=ss, lnv=lnv: e.activation(out=lnv, in_=ss, func=AF.Ln, scale=1.0 / D, bias=k.epsb[:, 0:1]), reads=[sbuf_], writes=[sbuf_])
                P.op("act", lambda e, rstd=rstd, lnv=lnv: e.activation(out=rstd, in_=lnv, func=AF.Exp, scale=-0.5), reads=[sbuf_], writes=[sbuf_])
                P.op("dve", lambda e, xa=xa, rstd=rstd, blk=blk, ot=ot: e.scalar_tensor_tensor(out=ot.ap[:, blk, :], in0=xa, scalar=rstd, in1=gfin[:],
                                                                                         op0=ALU.mult, op1=ALU.mult),
                     reads=[xt.buf, sbuf_], writes=[ot.buf])
            P.dma(k.out[t * 256:(t + 1) * 256, :].rearrange("(b p) c -> p b c", p=128), ot.ap, ot.sem, reads=[ot.buf])
        P.barrier()

def build(stage, limit_tiles=0, limit_groups=None, limit_j=0, limit3=0):
    nc = bass.Bass("TRN2", target_bir_lowering=False)
    k = K()
    k.nc = nc
    k.limit_tiles = limit_tiles
    k.limit_groups = limit_groups
    k.limit_j = limit_j
    k.limit3 = limit3
    io = {}

    def din(name, shape, dt=F32):
        io[name] = nc.dram_tensor(name, list(shape), dt, kind="ExternalInput").ap()

    din("xkv", (S, D))
    din("xq", (NQ, D))
    din("mem", (256, D))
    din("w_in", (D, INC))
    din("b_forget", (8, 1))
    din("rflag", (8, 1))
    din("lam4", (4, 64))
    din("g_subln", (1, 128))
    din("Lraw", (5, 2, 128, LW))
    din("t31", (5, 1))
    din("w_diff_out", (512, D))
    din("w_fox_out", (512, D))
    din("w_o", (D, D))
    din("g_mixT", (128, 8))
    din("g_mem_qT", (128, 8))
    din("g_mem_kvT", (128, 8))
    din("w_q_mem", (D, D))
    din("w_kv_mem", (D, 2 * D))
    din("w_o_mem", (D, D))
    din("g_mlpT", (128, 8))
    din("w1", (D, 4 * D))
    din("w2", (4 * D, D))
    din("g_final", (1, D))
    din("ident", (128, 128))
    k.io = io
    dbg = stage < 9
    skind = "ExternalOutput" if dbg else "Internal"
    k.KTd = nc.dram_tensor("KTd", [4, 128, S], BF16, kind=skind).ap()
    k.QTd = nc.dram_tensor("QTd", [4, 128, NQ], BF16, kind=skind).ap()
    k.KTf = nc.dram_tensor("KTf", [8, 68, S], BF16, kind=skind).ap()
    k.QTf = nc.dram_tensor("QTf", [8, 68, NQ], BF16, kind=skind).ap()
    k.Vs = nc.dram_tensor("Vs", [S, D], BF16, kind=skind).ap()
    k.SGs = nc.dram_tensor("SGs", [8, 2, 128, NQ], BF16, kind=skind).ap()
    k.out = nc.dram_tensor("out", [NQ, D], F32, kind="ExternalOutput").ap()
    k.X2s = nc.dram_tensor("X2s", [NQ, D], F32, kind=skind).ap()
    if stage == 2:
        k.Ydbg = nc.dram_tensor("Ydbg", [128, 32, D], BF16, kind="ExternalOutput").ap()

    with ExitStack() as st:
        P = Prog(nc, st)
        k.P = P
        k.ps = st.enter_context(nc.psum_tensor("ps", [128, 8, 512], F32))
        k.PB = [Buf(f"ps{i}") for i in range(8)]
        sb = mk_sb(nc, st)
        idf = sb("idf", [128, 128], F32)
        k.idb = sb("idb", [128, 128], BF16)
        k.epsb = sb("epsb", [128, 1], F32)
        k.oneb = sb("oneb", [128, 1], F32)
        k.junk = sb("junk", [128, 128], BF16)
        s0 = P.dsem("d_const")
        t = P.dma(idf[:], io["ident"], s0)
        P.wait("dve", t)
        t1 = P.op("dve", lambda e: e.tensor_copy(out=k.idb[:], in_=idf[:]))
        t2 = P.op("pool", lambda e: e.memset(k.epsb[:], EPS))
        t2 = P.op("pool", lambda e: e.memset(k.oneb[:], 1.0))
        wait_all(P, ["pe", "act", "dve"], [t1, t2])

        phase1(k, P)
        if stage >= 2:
            with ExitStack() as stY:
                Y = stY.enter_context(nc.sbuf_tensor("Yres", [128, 32, D], BF16))
                B_Y = Buf()
                phase2(k, P, Y, B_Y)
                if stage == 2:
                    sY = P.dsem("d_ydbg")
                    P.dma(k.Ydbg, Y[:], sY, reads=[B_Y])
                if stage >= 3:
                    KmT = stY.enter_context(nc.sbuf_tensor("KmT", [128, 8, 256], BF16))
                    Vm = stY.enter_context(nc.sbuf_tensor("Vm", [128, 2, 4, 258], BF16))
                    B_mem = Buf()
                    phaseM(k, P, KmT, Vm, B_mem)
                    phase3a(k, P, Y, B_Y, KmT, Vm, B_mem)
            if stage >= 3:
                phase3c(k, P)
        for s in P.dsems:
            if s.val:
                P.wait("sp", (s, s.val))
        P.emit()
    return nc


_NC_CACHE = {}


def t5_bucket_np(dist):
    dist = np.maximum(dist, 0)
    max_exact = 16
    d = np.maximum(dist, 1).astype(np.float32)
    large = max_exact + (np.log(d / np.float32(max_exact)) / np.float32(math.log(128 / max_exact)) * np.float32(32 - max_exact)).astype(np.int32)
    large = np.minimum(large, 31)
    return np.where(dist < max_exact, dist, large)


def build_Lraw(rel_bias, r):
    L = np.empty((5, 2, 128, LW), np.float32)
    kk = np.arange(128)[:, None]
    qq = np.arange(128)[None, :]
    for par in range(2):
        off = 4 if (r == 0) == (par == 0) else 8
        for u in range(15):
            delta = u - 11 + off
            dist = delta * 128 + qq - kk
            bucket = t5_bucket_np(dist)
            for h in range(5):
                if h < 4:
                    tile = rel_bias[bucket, h]
                else:
                    tile = np.zeros((128, 128), np.float32)
                tile = np.where(dist < 0, np.float32(MASKV), tile)
                L[h, par, :, u * 128:(u + 1) * 128] = tile
    return L


def make_inputs(c, x, mem, w_in, b_forget, lambda_q1, lambda_k1, lambda_q2, lambda_k2, g_subln, rel_bias,
                w_diff_out, w_fox_out, w_o, g_mix, g_mem_q, g_mem_kv, w_q_mem, w_kv_mem, w_o_mem, g_mlp, w1, w2, g_final):
    b, r = c // 2, c % 2
    f = lambda a: np.ascontiguousarray(np.asarray(a, dtype=np.float32))
    xb = f(x[b])
    tiles = own_tiles(r)
    xq = np.concatenate([xb[t * 512:(t + 1) * 512] for t in tiles], axis=0)
    gT = lambda g: f(np.asarray(g[0]).reshape(8, 128).T)
    t31 = np.zeros((5, 1), np.float32)
    t31[0:4, 0] = np.asarray(rel_bias)[31, :]
    return {
        "xkv": xb, "xq": f(xq), "mem": f(mem[b]), "w_in": f(w_in[0]),
        "b_forget": f(np.asarray(b_forget[0]).reshape(8, 1)),
        "rflag": np.full((8, 1), float(r), np.float32),
        "lam4": f(np.stack([lambda_q1[0], lambda_k1[0], lambda_q2[0], lambda_k2[0]])),
        "g_subln": f(np.asarray(g_subln[0]).reshape(1, 128)),
        "Lraw": build_Lraw(np.asarray(rel_bias, dtype=np.float32), r), "t31": t31,
        "w_diff_out": f(w_diff_out[0]), "w_fox_out": f(w_fox_out[0]), "w_o": f(w_o[0]),
        "g_mixT": gT(g_mix), "g_mem_qT": gT(g_mem_q), "g_mem_kvT": gT(g_mem_kv),
        "w_q_mem": f(w_q_mem[0]), "w_kv_mem": f(w_kv_mem[0]), "w_o_mem": f(w_o_mem[0]),
        "g_mlpT": gT(g_mlp), "w1": f(w1[0]), "w2": f(w2[0]),
        "g_final": f(np.asarray(g_final).reshape(1, D)),
        "ident": np.eye(128, dtype=np.float32),
    }


def kernel(**inputs):
    inputs = {k_: np.asarray(v) for k_, v in inputs.items()}
    if "nc" not in _NC_CACHE:
        _NC_CACHE["nc"] = build(9)
    nc = _NC_CACHE["nc"]
    in_maps = [make_inputs(c, **inputs) for c in range(8)]
    res = run_bass_kernel_spmd(nc, in_maps, core_ids=list(range(8)))
    out = np.empty((4, S, D), np.float32)
    for c in range(8):
        b, r = c // 2, c % 2
        o = np.asarray(res.results[c]["out"], dtype=np.float32)
        for j, t in enumerate(own_tiles(r)):
            out[b, t * 512:(t + 1) * 512] = o[j * 512:(j + 1) * 512]
    return out
```
